# Optimizing a Trainium2 kernel written in Bass

```python
import jax, jax.numpy as jnp
from jax import lax
import numpy as np

D_MODEL = 1024
BATCH = 2
SEQ = 8192
DEPTH = 1

CHUNK = 64
N_META = 16
Q_BLOCK = 128
SB_HEADS = 8
SB_HEAD_DIM = 64
SB_WIDTH = SB_HEADS * SB_HEAD_DIM
SB_SCALE = SB_HEAD_DIM ** -0.5
HG_HEADS = 4
HG_KEY_DIM = 128
HG_VAL_DIM = 128
HG_KEY_WIDTH = HG_HEADS * HG_KEY_DIM
HG_VAL_WIDTH = HG_HEADS * HG_VAL_DIM
HG_SCALE = HG_KEY_DIM ** -0.5
N_BRANCH = 2
SPLIT_SIZES = (SB_WIDTH, SB_WIDTH, SB_WIDTH, SB_WIDTH,
               HG_KEY_WIDTH, HG_KEY_WIDTH, HG_VAL_WIDTH, HG_VAL_WIDTH,
               D_MODEL, D_MODEL)
IN_COLS = sum(SPLIT_SIZES)
EPS = 1e-6

kernel_name = "hybrid_stickbreak_hgrn2_block"


def _rmsnorm(x, g):
    xf = x.astype(jnp.float32)
    y = xf * lax.rsqrt(jnp.mean(xf * xf, axis=-1, keepdims=True) + EPS)
    return (y * g.astype(jnp.float32)).astype(x.dtype)


def _group_rmsnorm(o, g):
    B, L, _ = o.shape
    oh = o.reshape(B, L, HG_HEADS, HG_VAL_DIM)
    oh = oh * lax.rsqrt(jnp.mean(oh * oh, axis=-1, keepdims=True) + EPS)
    return oh.reshape(B, L, HG_VAL_WIDTH) * g.astype(jnp.float32)


def _stick_breaking(q, k, v):
    B, L, _ = q.shape

    def heads(a):
        return a.astype(jnp.float32).reshape(B, L, SB_HEADS, SB_HEAD_DIM).transpose(0, 2, 1, 3)

    q, k, v = heads(q) * SB_SCALE, heads(k), heads(v)
    nb = L // Q_BLOCK
    q_blocks = q.reshape(B, SB_HEADS, nb, Q_BLOCK, SB_HEAD_DIM).transpose(2, 0, 1, 3, 4)
    starts = jnp.arange(nb, dtype=jnp.int32) * Q_BLOCK
    s_idx = jnp.arange(L, dtype=jnp.int32)

    def block(args):
        qb, t0 = args
        z = jnp.einsum('bhtd,bhsd->bhts', qb, k)
        t_idx = t0 + jnp.arange(Q_BLOCK, dtype=jnp.int32)
        valid = s_idx[None, :] < t_idx[:, None]
        log_beta = jax.nn.log_sigmoid(z)
        log_1mb = jnp.where(valid, jax.nn.log_sigmoid(-z), 0.0)
        later = lax.cumsum(log_1mb, axis=3, reverse=True) - log_1mb
        weights = jnp.where(valid, jnp.exp(log_beta + later), 0.0)
        return jnp.einsum('bhts,bhsd->bhtd', weights, v)

    out = lax.map(block, (q_blocks, starts))
    return out.transpose(1, 0, 3, 2, 4).reshape(B, L, SB_WIDTH)


def _hgrn2(q, f_logit, i, lb):
    B, L, _ = q.shape
    n = L // CHUNK
    f = lb + (1.0 - lb) * jax.nn.sigmoid(f_logit.astype(jnp.float32))
    kk = 1.0 - f
    g = jnp.log(f)
    qf = jax.nn.silu(q.astype(jnp.float32)) * HG_SCALE

    def chunks(a, dh):
        return a.reshape(B, n, CHUNK, HG_HEADS, dh).transpose(1, 0, 3, 2, 4)

    xs = (chunks(qf, HG_KEY_DIM), chunks(kk, HG_KEY_DIM),
          chunks(i.astype(jnp.float32), HG_VAL_DIM), chunks(g, HG_KEY_DIM))
    pos = jnp.arange(CHUNK)
    causal = pos[:, None] >= pos[None, :]

    def step(S, c):
        qc, kc, vc, gc = c
        b = jnp.cumsum(gc, axis=2)
        diff = b[:, :, :, None, :] - b[:, :, None, :, :]
        decay = jnp.exp(jnp.where(causal[:, :, None], diff, -jnp.inf))
        A = jnp.einsum('bhtk,bhtsk,bhsk->bhts', qc, decay, kc)
        o = (jnp.einsum('bhts,bhsv->bhtv', A, vc)
             + jnp.einsum('bhtk,bhkv->bhtv', qc * jnp.exp(b), S))
        b_last = b[:, :, -1:, :]
        S = (jnp.exp(b_last[:, :, 0, :])[..., None] * S
             + jnp.einsum('bhsk,bhsv->bhkv', kc * jnp.exp(b_last - b), vc))
        return S, o

    S0 = jnp.zeros((B, HG_HEADS, HG_KEY_DIM, HG_VAL_DIM), jnp.float32)
    _, o = lax.scan(step, S0, xs)
    return o.transpose(1, 0, 3, 2, 4).reshape(B, L, HG_VAL_WIDTH)


def setup_inputs(seed: int = 0) -> dict:
    key = jax.random.key(seed)
    ks = jax.random.split(key, 10)
    f32 = jnp.float32
    x = jax.random.normal(ks[0], (BATCH, SEQ, D_MODEL), f32)
    meta = jax.random.normal(ks[1], (N_META, D_MODEL), f32)
    norm_g = 1.0 + 0.02 * jax.random.normal(ks[2], (DEPTH, D_MODEL), f32)
    w_in = jax.random.normal(ks[3], (DEPTH, D_MODEL, IN_COLS), f32) * D_MODEL ** -0.5
    w_sb_out = jax.random.normal(ks[4], (DEPTH, SB_WIDTH, D_MODEL), f32) * SB_WIDTH ** -0.5
    w_hg_out = jax.random.normal(ks[5], (DEPTH, HG_VAL_WIDTH, D_MODEL), f32) * HG_VAL_WIDTH ** -0.5
    w_out = jax.random.normal(ks[6], (DEPTH, D_MODEL, D_MODEL), f32) * D_MODEL ** -0.5
    hg_norm_g = 1.0 + 0.02 * jax.random.normal(ks[7], (DEPTH, HG_VAL_WIDTH), f32)
    hg_lb_logits = 0.1 * jax.random.normal(ks[8], (DEPTH + 1, HG_KEY_WIDTH), f32)
    final_norm_g = 1.0 + 0.02 * jax.random.normal(ks[9], (D_MODEL,), f32)
    return {"x": x, "meta": meta, "norm_g": norm_g, "w_in": w_in, "w_sb_out": w_sb_out,
            "w_hg_out": w_hg_out, "w_out": w_out, "hg_norm_g": hg_norm_g,
            "hg_lb_logits": hg_lb_logits, "final_norm_g": final_norm_g}


def reference(x, meta, norm_g, w_in, w_sb_out, w_hg_out, w_out, hg_norm_g, hg_lb_logits, final_norm_g):
    B, S, D = x.shape
    dtype = x.dtype
    L = N_META + S
    Lp = ((L + Q_BLOCK - 1) // Q_BLOCK) * Q_BLOCK
    h = jnp.concatenate([jnp.broadcast_to(meta.astype(dtype)[None], (B, N_META, D)), x], axis=1)
    h = jnp.pad(h, ((0, 0), (0, Lp - L), (0, 0)))

    lb_all = jnp.cumsum(jax.nn.softmax(hg_lb_logits.astype(jnp.float32), axis=0), axis=0)
    bounds = np.cumsum(SPLIT_SIZES)[:-1].tolist()

    for l in range(DEPTH):
        u = _rmsnorm(h, norm_g[l])
        proj = u @ w_in[l]
        (sb_q, sb_k, sb_v, sb_gate, hg_q, hg_f, hg_i, hg_gate,
         gate_sb, gate_hg) = jnp.split(proj, bounds, axis=-1)

        y_sb = (_stick_breaking(sb_q, sb_k, sb_v) * jax.nn.silu(sb_gate.astype(jnp.float32))).astype(dtype)
        o_hg = _group_rmsnorm(_hgrn2(hg_q, hg_f, hg_i, lb_all[l]), hg_norm_g[l])
        y_hg = (o_hg * jax.nn.silu(hg_gate.astype(jnp.float32))).astype(dtype)

        merged = (jax.nn.sigmoid(gate_sb) * (y_sb @ w_sb_out[l])
                  + jax.nn.sigmoid(gate_hg) * (y_hg @ w_hg_out[l]))
        h = h + merged @ w_out[l]

    return _rmsnorm(h, final_norm_g)[:, N_META:N_META + S]
```

```python
import os
import numpy as np
import ml_dtypes
from contextlib import ExitStack
import concourse.bass as bass
import concourse.mybir as mybir
from concourse.bass_utils import run_bass_kernel_spmd

F32 = mybir.dt.float32
BF16 = mybir.dt.bfloat16
AF = mybir.ActivationFunctionType
ALU = mybir.AluOpType
AX = mybir.AxisListType

D = 1024
EPS = 1e-6
SB_SCALE = 64 ** -0.5
HG_SCALE = 128 ** -0.5
C_SBQ, C_SBK, C_SBV, C_SBG, C_HGQ, C_HGF, C_HGI, C_HGG, C_GSB, C_GHG = (
    0, 512, 1024, 1536, 2048, 2560, 3072, 3584, 4096, 5120)


class Sync:
    ENG = ("pe", "act", "dve", "pool", "sp")

    def __init__(self, nc, es):
        self.nc = nc
        self.es = es
        self.sem = {e: es.enter_context(nc.semaphore("s_" + e)) for e in self.ENG}
        self.cnt = {e: 0 for e in self.ENG}
        self.prog = {e: [] for e in self.ENG}
        self.waited = {e: {} for e in self.ENG}
        self.bufs = {}
        self.dma_sems = {}
        self.alias = {}

    def _expand(self, keys):
        out = []
        for k in keys:
            out.extend(self.alias.get(k, (k,)))
        return out

    def _buf(self, k):
        b = self.bufs.get(k)
        if b is None:
            b = self.bufs[k] = ({}, {})
        return b

    def _deps(self, eng, reads, writes):
        deps = {}

        def add(d):
            for s, (sem, v) in d.items():
                if deps.get(s, (None, -1))[1] < v:
                    deps[s] = (sem, v)
        for k in reads:
            add(self._buf(k)[0])
        for k in writes:
            w, r = self._buf(k)
            add(w)
            add(r)
        for s, (sem, v) in deps.items():
            if eng == "pe" and s == "pe":
                continue
            if self.waited[eng].get(s, -1) >= v:
                continue
            self.waited[eng][s] = v
            self.prog[eng].append(("wait", sem, v))

    def _commit(self, tokname, tok, reads, writes):
        for k in reads:
            self._buf(k)[1][tokname] = tok
        for k in writes:
            w, r = self._buf(k)
            w[tokname] = tok
            r.clear()

    def op(self, eng, meth, r, w, *a, **k):
        r = self._expand(r)
        self._deps(eng, r, w)
        self.cnt[eng] += 1
        sem = self.sem[eng]
        self.prog[eng].append(("op", (meth, a, k), sem, 1))
        self._commit(eng, (sem, self.cnt[eng]), r, w)

    def dma(self, slot, r, w, out, in_, eng="sp"):
        if slot not in self.dma_sems:
            self.dma_sems[slot] = [self.es.enter_context(self.nc.semaphore("d_" + slot)), 0]
        self._deps(eng, r, w)
        ds = self.dma_sems[slot]
        ds[1] += 16
        self.prog[eng].append(("op", ("dma_start", (), dict(out=out, in_=in_)), ds[0], 16))
        self._commit("dma_" + slot, (ds[0], ds[1]), r, w)

    def barrier_all(self):
        toks = {e: (self.sem[e], self.cnt[e]) for e in self.ENG if self.cnt[e] > 0}
        for s, ds in self.dma_sems.items():
            toks["dma_" + s] = (ds[0], ds[1])
        for e in self.ENG:
            for s, (sem, v) in toks.items():
                if s == e or v == 0:
                    continue
                if self.waited[e].get(s, -1) >= v:
                    continue
                self.waited[e][s] = v
                self.prog[e].append(("wait", sem, v))
        self.bufs = {}

    def emit(self):
        nc = self.nc
        emap = {"pe": "tensor", "act": "scalar", "dve": "vector", "pool": "gpsimd", "sp": "sync"}
        with nc.Block() as block:
            for e in self.ENG:
                prog = self.prog[e]

                def body(engine, prog=prog):
                    for it in prog:
                        if it[0] == "wait":
                            engine.wait_ge(it[1], it[2])
                        else:
                            meth, a, k = it[1]
                            getattr(engine, meth)(*a, **k).then_inc(it[2], it[3])
                getattr(block, emap[e])(body)


def host_consts():
    s = np.arange(128)[:, None]
    t = np.arange(128)[None, :]
    same = (s // 64) == (t // 64)
    c = {}
    c["ident"] = np.eye(128).astype(ml_dtypes.bfloat16)
    c["triC"] = (same & (s <= t)).astype(np.float32)
    c["triSU"] = (same & (s > t)).astype(np.float32)
    c["triK"] = (s >= t).astype(ml_dtypes.bfloat16)
    c["ones"] = np.ones((128, 128), ml_dtypes.bfloat16)
    dm = np.zeros((128, 4, 4, 128), np.float32)
    for cp in range(4):
        for cq in range(4):
            if cp < cq:
                dm[:, cp, cq, :] = 1.0
            elif cp == cq:
                dm[:, cp, cq, :] = (s < t)
    c["dmask"] = dm.reshape(128, 4, 512).astype(ml_dtypes.bfloat16)
    return c


def build(NG, OWN):
    NOWN = len(OWN)
    T = NG * 512
    NB = NG * 4
    nc = bass.Bass("TRN2", target_bir_lowering=False)

    def din(name, shape, dt=F32):
        return nc.dram_tensor(name, list(shape), dt, kind="ExternalInput").ap()
    xloc = din("xloc", [T, D])
    w_in = din("w_in", [D, 6144])
    w_sbo = din("w_sb_out", [512, D])
    w_hgo = din("w_hg_out", [512, D])
    w_out = din("w_out", [D, D])
    ngT_d = din("ngT", [128, 8])
    hgn_d = din("hgn_bc", [128, 512])
    fng_d = din("fng_bc", [128, D])
    lbl_d = din("lbl_bc", [128, 2, 512])
    lblT_d = din("lblT", [128, 2, 4])
    ident_d = din("ident", [128, 128], BF16)
    triC_d = din("triC", [128, 128])
    triSU_d = din("triSU", [128, 128])
    triK_d = din("triK", [128, 128], BF16)
    ones_d = din("ones", [128, 128], BF16)
    dmask_d = din("dmask", [128, 4, 512], BF16)
    out_d = nc.dram_tensor("out", [NOWN * 512, D], F32, kind="ExternalOutput").ap()

    with ExitStack() as es:
        S = Sync(nc, es)
        OP = S.op

        es_mem = ExitStack()

        def sb(name, shape, dt, scope=es_mem):
            return scope.enter_context(nc.sbuf_tensor(name, list(shape), dt))

        PS01 = es_mem.enter_context(nc.psum_tensor("ps01", [128, 2, 512], F32))
        PS23 = es_mem.enter_context(nc.psum_tensor("ps23", [128, 2, 512], F32))
        PS = [PS01[:, 0, :], PS01[:, 1, :], PS23[:, 0, :], PS23[:, 1, :]]
        PS += [es_mem.enter_context(nc.psum_tensor("ps%d" % i, [128, 512], F32))[:] for i in range(4, 7)]
        PTB = es_mem.enter_context(nc.psum_tensor("ptb", [128, 8, 128], BF16))
        PSZ = [PS01, PS23]

        def pk(i):
            return ("ps", i)

        def cload(name, shape, dt, src, scope=None):
            scope = es_mem if scope is None else scope
            tl = sb(name + "_s", shape, dt, scope)
            S.dma("c_" + name, [], [name], tl[:], src)
            return tl

        ident = cload("ident", [128, 128], BF16, ident_d)
        triC = cload("triC", [128, 128], F32, triC_d)
        triSU = cload("triSU", [128, 128], F32, triSU_d)
        triK = cload("triK", [128, 128], BF16, triK_d)
        ones = cload("ones", [128, 128], BF16, ones_d)
        dmask = cload("dmask", [128, 4, 512], BF16, dmask_d)
        ngT = cload("ngT", [128, 8], F32, ngT_d)

        yT = sb("yT", [128, 4, NOWN * 512], BF16)
        onall = sb("onall", [128, NOWN * 4, 512], BF16)

        NXS = 3
        xt = [sb("xt%d" % i, [128, D], F32) for i in range(NXS)]
        junkd = sb("junkd", [128, D], BF16)
        xs = [sb("xs%d" % i, [128, D], BF16) for i in range(2)]
        uT = [sb("uT%d" % i, [128, 8, 513], BF16) for i in range(2)]
        st = [sb("st%d" % i, [128, 4], F32) for i in range(NXS)]
        NWS = 4
        WCH = 512
        wst = [sb("wst%d" % i, [128, WCH], F32) for i in range(NWS)]
        wst_i = [0]
        x_i = [0]

        def load_w(dst, dkey, src, row0, col0, ncols, gcol=None):
            i = wst_i[0] % NWS
            on_act = (wst_i[0] % 2) == 1
            wst_i[0] += 1
            if dkey not in S.alias:
                S.alias[dkey] = [(dkey, 0), (dkey, 1)]
            dkey = (dkey, 1 if on_act else 0)
            S.dma("wst%d" % i, [], ["wst%d" % i], wst[i][:, 0:ncols], src[row0:row0 + 128, col0:col0 + ncols])
            if gcol is None:
                if on_act:
                    OP("act", "copy", ["wst%d" % i], [dkey], out=dst, in_=wst[i][:, 0:ncols])
                else:
                    OP("dve", "tensor_copy", ["wst%d" % i], [dkey], out=dst, in_=wst[i][:, 0:ncols])
            else:
                if on_act:
                    OP("act", "activation", ["wst%d" % i, "ngT"], [dkey], out=dst, in_=wst[i][:, 0:ncols],
                       func=AF.Copy, scale=gcol)
                else:
                    OP("dve", "tensor_scalar", ["wst%d" % i, "ngT"], [dkey], out=dst, in0=wst[i][:, 0:ncols],
                       scalar1=gcol, scalar2=None, op0=ALU.mult)

        def load_win(Wt, wkey, off, col0, ncols):
            for c in range(8):
                for q0 in range(0, ncols, WCH):
                    n = min(WCH, ncols - q0)
                    load_w(Wt[:, c, off + q0:off + q0 + n], wkey, w_in, c * 128, col0 + q0, n, gcol=ngT[:, c:c + 1])

        def rstd_chain(src, srckey, stt, sttkey, n=D):
            OP("dve", "scalar_tensor_tensor", [srckey], [sttkey], out=junkd[:, 0:n], in0=src, scalar=1.0, in1=src,
               op0=ALU.mult, op1=ALU.mult, accum_out=stt[:, 0:1])
            OP("dve", "tensor_scalar", [sttkey], [sttkey], out=stt[:, 1:2], in0=stt[:, 0:1], scalar1=1.0 / n,
               scalar2=EPS, op0=ALU.mult, op1=ALU.add)
            OP("act", "activation", [sttkey], [sttkey], out=stt[:, 2:3], in_=stt[:, 1:2], func=AF.Ln)
            OP("act", "activation", [sttkey], [sttkey], out=stt[:, 3:4], in_=stt[:, 2:3], func=AF.Exp, scale=-0.5)

        def load_x(row0):
            i = x_i[0] % NXS
            x_i[0] += 1
            S.dma("x%d" % i, [], ["xt%d" % i], xt[i][:], xloc[row0:row0 + 128, :])
            return i

        def token_group_parts(g, gi):
            ub = gi % 2
            ut = uT[ub]
            ukey = "uT%d" % ub
            slot = {}

            def st_load(tb):
                slot[tb] = load_x(g * 512 + tb * 128)

            def st_stats(tb):
                i = slot[tb]
                rstd_chain(xt[i][:], "xt%d" % i, st[i], "st%d" % i)

            def st_rest(tb):
                i = slot[tb]
                j = tb % 2
                OP("dve", "tensor_scalar", ["xt%d" % i, "st%d" % i], ["xs%d" % j], out=xs[j][:], in0=xt[i][:],
                   scalar1=st[i][:, 3:4], scalar2=None, op0=ALU.mult)
                for c in range(8):
                    OP("pe", "transpose", ["xs%d" % j, "ident"], ["ptb"], out=PTB[:, c, :],
                       in_=xs[j][:, c * 128:(c + 1) * 128], identity=ident[:])
                OP("act", "copy", ["ptb"], [ukey], out=ut[:, :, 1 + tb * 128:1 + (tb + 1) * 128], in_=PTB[:])
            def part(tb):
                if tb == 0:
                    st_load(0)
                    st_load(1)
                    st_stats(0)
                if tb + 2 < 4:
                    st_load(tb + 2)
                if tb + 1 < 4:
                    st_stats(tb + 1)
                st_rest(tb)
                if tb == 3:
                    if gi == 0:
                        OP("pool", "memset", [], [ukey], ut[:, :, 0:1], 0.0)
                    else:
                        pv = uT[(gi - 1) % 2]
                        OP("pool", "tensor_copy", ["uT%d" % ((gi - 1) % 2)], [ukey], out=ut[:, :, 0:1],
                           in_=pv[:, :, 512:513])
            return ut, ukey, [(lambda tb=tb: part(tb)) for tb in range(4)]

        def token_group(g, gi):
            ut, ukey, parts = token_group_parts(g, gi)
            for p_ in parts:
                p_()
            return ut, ukey

        def interleave(nxt, quarters):
            for q in range(4):
                if nxt is not None:
                    nxt[2][q]()
                for f in quarters[q]:
                    f()

        def proj_fm(bank, Wt, wkey, c0, ut, ukey, shift=0, ncols=128):
            for c in range(8):
                OP("pe", "matmul", [wkey, ukey], [pk(bank)], PS[bank][0:ncols, :], lhsT=Wt[:, c, c0:c0 + ncols],
                   rhs=ut[:, c, 1 - shift:513 - shift], start=(c == 0), stop=(c == 7))

        def proj_tm(bank, src, skey, off, tb, Wt, wkey, c0, ncols):
            for c in range(8):
                OP("pe", "matmul", [wkey, skey], [pk(bank)], PS[bank][:, 0:ncols],
                   lhsT=src[:, c, off + tb * 128:off + (tb + 1) * 128], rhs=Wt[:, c, c0:c0 + ncols],
                   start=(c == 0), stop=(c == 7))

        def recip1p(tl, key):
            OP("act", "activation", [key], [key], out=tl, in_=tl, func=AF.Ln, bias=1.0)
            OP("act", "activation", [key], [key], out=tl, in_=tl, func=AF.Exp, scale=-1.0)

        SKIP = os.environ.get('KSKIP', '')
        def pass_H():
            with ExitStack() as hs:
                WH = sb("WH", [128, 8, 1536], BF16, hs)
                load_win(WH, "WH", 0, C_HGQ, 1536)
                hgn = cload("hgn", [128, 512], F32, hgn_d, hs)
                lbl = cload("lbl", [128, 2, 512], F32, lbl_d, hs)
                lblT = cload("lblT", [128, 2, 4], F32, lblT_d, hs)
                oml = sb("oml", [128, 512], F32, hs)
                omlT = sb("omlT", [128, 4], F32, hs)
                for (dst, dk, src, sk) in ((oml, "oml", lbl, "lbl"), (omlT, "omlT", lblT, "lblT")):
                    OP("dve", "tensor_tensor", [sk], [dk], out=dst[:], in0=src[:, 0, :], in1=src[:, 1, :], op=ALU.subtract)
                    OP("act", "activation", [dk], [dk], out=dst[:], in_=dst[:], func=AF.Exp)
                    recip1p(dst[:], dk)

                tA = [sb("h_tA%d" % i, [128, 512], F32, hs) for i in range(4)]
                tB = [sb("h_tB%d" % i, [128, 512], F32, hs) for i in range(4)]
                tC = [sb("h_tC%d" % i, [128, 512], F32, hs) for i in range(4)]
                tD = [sb("h_tD%d" % i, [128, 512], F32, hs) for i in range(4)]
                kk = [sb("h_kk%d" % i, [128, 512], F32, hs) for i in range(4)]
                gtok = sb("h_g", [128, 4, 512], F32, hs)
                kdl2 = [sb("h_kdl%d" % i, [128, 4, 512], BF16, hs) for i in range(2)]
                vc2 = [sb("h_vc%d" % i, [128, 4, 512], BF16, hs) for i in range(2)]
                kdT2 = [sb("h_kdT%d" % i, [128, 4, 512], BF16, hs) for i in range(2)]
                dec2 = [sb("h_dec%d" % i, [128, 4, 8], F32, hs) for i in range(2)]
                qz = sb("h_qz", [128, 4, 4, 2, 128], BF16, hs)
                Sst = sb("h_S", [128, 4, 128], F32, hs)
                Sbf = sb("h_Sbf", [128, 2, 4, 2, 128], BF16, hs)
                atm4 = [sb("h_atm%d" % i, [128, 4, 128], BF16, hs) for i in range(2)]
                osb4 = [sb("h_o%d" % i, [128, 512], F32, hs) for i in range(2)]
                ost4 = [sb("h_ost%d" % i, [128, 16], F32, hs) for i in range(2)]
                triC4 = sb("h_triC4", [128, 4, 128], F32, hs)
                for h in range(4):
                    OP("pool", "tensor_copy", ["triC"], ["triC4"], out=triC4[:, h, :], in_=triC[:])
                OP("pool", "memset", [], [("qz", h) for h in range(4)], qz[:], 0.0)
                OP("pool", "memset", [], [("S", h) for h in range(4)], Sst[:], 0.0)

                cur = token_group_parts(0, 0)
                if 'H' not in SKIP:
                    for p_ in cur[2]:
                        p_()
                for g in range(NG if 'H' not in SKIP else 0):
                    own = g in OWN
                    oj = OWN.index(g) if own else -1
                    ut, ukey = cur[0], cur[1]
                    nxt = token_group_parts(g + 1, g + 1) if g + 1 < NG else None
                    gp = g % 2
                    kdl, vc, kdT, dec = kdl2[gp], vc2[gp], kdT2[gp], dec2[gp]
                    K_vc = lambda tb: ("vc", gp, tb)
                    K_kdl = lambda tb: ("kdl", gp, tb)
                    K_kdT = lambda h: ("kdT", gp, h)
                    K_dec = lambda h: ("dec", gp, h)
                    def part_T1():
                        for tb in range(4):
                            proj_tm(0, ut, ukey, 1, tb, WH, "WH", 512, 512)
                            proj_tm(1, ut, ukey, 1, tb, WH, "WH", 1024, 512)
                            ka = "tA%d" % tb
                            OP("act", "activation", [pk(0)], [ka], out=tA[tb][:], in_=PS[0][:], func=AF.Exp)
                            OP("act", "copy", [pk(1)], [K_vc(tb)], out=vc[:, tb, :], in_=PS[1][:])
                            recip1p(tA[tb][:], ka)
                            OP("dve", "tensor_tensor", [ka, "oml"], ["kk%d" % tb], out=kk[tb][:], in0=tA[tb][:], in1=oml[:],
                               op=ALU.mult)
                    def part_T23():
                        for tb in range(4):
                            OP("act", "activation", ["kk%d" % tb], [("gtok", tb)], out=gtok[:, tb, :], in_=kk[tb][:], func=AF.Ln,
                               scale=-1.0, bias=1.0)
                        for tb in range(4):
                            OP("pe", "matmul", ["triSU", ("gtok", tb)], [pk(2)], PS[2][:], lhsT=triSU[:], rhs=gtok[:, tb, :],
                               start=True, stop=True)
                            OP("act", "activation", [pk(2)], ["tB%d" % tb], out=tB[tb][:], in_=PS[2][:], func=AF.Exp)
                            OP("dve", "tensor_tensor", ["kk%d" % tb, "tB%d" % tb], [K_kdl(tb)], out=kdl[:, tb, :], in0=kk[tb][:],
                               in1=tB[tb][:], op=ALU.mult)
                            for h in range(4):
                                OP("pe", "matmul", ["triC", ("gtok", tb)], [pk(3 + h)], PS[3 + h][:, tb * 128:(tb + 1) * 128],
                                   lhsT=gtok[:, tb, h * 128:(h + 1) * 128], rhs=triC[:], start=True, stop=True)
                    def part_F():
                        for h in range(4):
                            proj_fm(0, WH, "WH", 512 + h * 128, ut, ukey)
                            ka, kb_ = "tA%d" % h, "tB%d" % h
                            OP("act", "activation", [pk(0)], [ka], out=tA[h][:], in_=PS[0][:], func=AF.Exp)
                            recip1p(tA[h][:], ka)
                            OP("act", "activation", [pk(3 + h)], [kb_], out=tB[h][:], in_=PS[3 + h][:], func=AF.Exp, scale=-1.0)
                            OP("dve", "scalar_tensor_tensor", [ka, kb_, "omlT"], [K_kdT(h)], out=kdT[:, h, :], in0=tA[h][:],
                               scalar=omlT[:, h:h + 1], in1=tB[h][:], op0=ALU.mult, op1=ALU.mult)
                            OP("act", "activation", [pk(3 + h)], [K_dec(h)], out=dec[:, h, :],
                               in_=PS[3 + h][:].rearrange("p (c t) -> p c t", t=64)[:, :, 63], func=AF.Exp)
                        if own:
                            for h in range(4):
                                qb = h % 2
                                kc, kd = "tC%d" % h, "tD%d" % h
                                proj_fm(qb, WH, "WH", h * 128, ut, ukey)
                                OP("act", "activation", [pk(qb)], [kc], out=tC[h][:], in_=PS[qb][:], func=AF.Exp, scale=-1.0)
                                recip1p(tC[h][:], kc)
                                OP("dve", "tensor_tensor", [pk(qb), kc], [kc], out=tC[h][:], in0=PS[qb][:], in1=tC[h][:],
                                   op=ALU.mult)
                                OP("act", "activation", [pk(3 + h)], [kd], out=tD[h][:], in_=PS[3 + h][:], func=AF.Exp)
                                for ci in range(2):
                                    v4c = tC[h][:].rearrange("p (b c t) -> p b c t", b=4, c=2)[:, :, ci, :]
                                    v4d = tD[h][:].rearrange("p (b c t) -> p b c t", b=4, c=2)[:, :, ci, :]
                                    OP("dve", "scalar_tensor_tensor", [kc, kd], [("qz", h)],
                                       out=qz[:, h, :, ci, ci * 64:(ci + 1) * 64], in0=v4c, scalar=HG_SCALE, in1=v4d,
                                       op0=ALU.mult, op1=ALU.mult)
                    def part_R():
                        for cch in range(8):
                            tb, ci = cch // 2, cch % 2
                            for h in range(4):
                                hc = slice(h * 128, (h + 1) * 128)
                                rows = slice(ci * 64, (ci + 1) * 64)
                                if own:
                                    OP("pool", "tensor_copy", [("S", h)], [("Sbf", h, tb % 2)], out=Sbf[:, tb % 2, h, ci, :],
                                       in_=Sst[:, h, :])
                                OP("pe", "matmul", [K_kdl(tb), K_vc(tb)], [pk(3 + h)], PS[3 + h][:, 0:128], lhsT=kdl[rows, tb, hc],
                                   rhs=vc[rows, tb, hc], start=True, stop=True)
                                OP("dve", "scalar_tensor_tensor", [("S", h), K_dec(h), pk(3 + h)], [("S", h)], out=Sst[:, h, :],
                                   in0=Sst[:, h, :], scalar=dec[:, h, cch:cch + 1], in1=PS[3 + h][:, 0:128],
                                   op0=ALU.mult, op1=ALU.add)
                            if own and ci == 1:
                                blk = slice(tb * 128, (tb + 1) * 128)
                                pb = tb % 2
                                atm_, osb_, ost_ = atm4[pb], osb4[pb], ost4[pb]
                                ak, okk, sk = "atm%d" % pb, "osb%d" % pb, "ost%d" % pb
                                for h in range(4):
                                    reg = PS[2][:, h * 128:(h + 1) * 128]
                                    OP("pe", "matmul", [K_kdT(h), ("qz", h)], [pk(2)], reg, lhsT=kdT[:, h, blk],
                                       rhs=qz[:, h, tb, 0, :], start=True, stop=False)
                                    OP("pe", "matmul", [K_kdT(h), ("qz", h)], [pk(2)], reg, lhsT=kdT[:, h, blk],
                                       rhs=qz[:, h, tb, 1, :], start=False, stop=True)
                                OP("dve", "tensor_tensor", [pk(2), "triC4"], [ak], out=atm_[:],
                                   in0=PS[2][:].rearrange("p (h t) -> p h t", h=4), in1=triC4[:], op=ALU.mult)
                                for h in range(4):
                                    hc = slice(h * 128, (h + 1) * 128)
                                    reg = PS[1][:, hc]
                                    OP("pe", "matmul", [ak, K_vc(tb)], [pk(1)], reg, lhsT=atm_[:, h, :], rhs=vc[:, tb, hc],
                                       start=True, stop=False)
                                    OP("pe", "matmul", [("qz", h), ("Sbf", h, pb)], [pk(1)], reg, lhsT=qz[:, h, tb, 0, :],
                                       rhs=Sbf[:, pb, h, 0, :], start=False, stop=False)
                                    OP("pe", "matmul", [("qz", h), ("Sbf", h, pb)], [pk(1)], reg, lhsT=qz[:, h, tb, 1, :],
                                       rhs=Sbf[:, pb, h, 1, :], start=False, stop=True)
                                OP("act", "copy", [pk(1)], [okk], out=osb_[:], in_=PS[1][:])
                                for h in range(4):
                                    hc = slice(h * 128, (h + 1) * 128)
                                    OP("dve", "scalar_tensor_tensor", [okk], [sk], out=junkd[:, 0:128], in0=osb_[:, hc],
                                       scalar=1.0, in1=osb_[:, hc], op0=ALU.mult, op1=ALU.mult, accum_out=ost_[:, h:h + 1])
                                OP("dve", "tensor_scalar", [sk], [sk], out=ost_[:, 4:8], in0=ost_[:, 0:4], scalar1=1.0 / 128,
                                   scalar2=EPS, op0=ALU.mult, op1=ALU.add)
                                OP("act", "activation", [sk], [sk], out=ost_[:, 8:12], in_=ost_[:, 4:8], func=AF.Ln)
                                OP("act", "activation", [sk], [sk], out=ost_[:, 12:16], in_=ost_[:, 8:12], func=AF.Exp,
                                   scale=-0.5)
                                for h in range(4):
                                    hc = slice(h * 128, (h + 1) * 128)
                                    OP("dve", "scalar_tensor_tensor", [okk, sk, "hgn"], ["onall"],
                                       out=onall[:, oj * 4 + tb, hc], in0=osb_[:, hc], scalar=ost_[:, 12 + h:13 + h],
                                       in1=hgn[:, hc], op0=ALU.mult, op1=ALU.mult)
                    interleave(nxt, [[part_T1], [part_T23], [part_F], [part_R]])
                    cur = nxt
                S.barrier_all()


        def pass_A():
            with ExitStack() as as_:
                WA = sb("WA", [128, 8, 768], BF16, as_)
                KT = sb("KT", [128, 2, T], BF16, as_)
                DV = sb("DV", [128, NB, 256], BF16, as_)
                QT = sb("QT", [128, 2, NOWN, 512], BF16, as_)
                VsT = sb("VsT", [128, 2, NOWN, 512], BF16, as_)
                duT = sb("duT", [128, 8, 512], BF16, as_)
                Et = [sb("E%d" % p, [128, 2, 512], F32, as_) for p in range(2)]
                spt = [sb("sp%d" % p, [128, 2, 512], BF16, as_) for p in range(2)]
                Pt = [sb("P%d" % p, [128, 2, 512], BF16, as_) for p in range(2)]
                Rt = [sb("R%d" % p, [128, 2, 512], BF16, as_) for p in range(2)]

                for hp in range(2 if 'A' not in SKIP else 0):
                    h0 = hp * 4
                    load_win(WA, "WA", 0, C_SBQ + h0 * 64, 256)
                    load_win(WA, "WA", 256, C_SBK + h0 * 64, 256)
                    load_win(WA, "WA", 512, C_SBV + h0 * 64, 256)
                    bank_i = [0]

                    def nb():
                        bank_i[0] = (bank_i[0] + 1) % 4
                        return bank_i[0]
                    cur = token_group_parts(0, 0)
                    for p_ in cur[2]:
                        p_()
                    for g in range(NG):
                        own = g in OWN
                        oj = OWN.index(g) if own else -1
                        ut, ukey = cur[0], cur[1]
                        nxt = token_group_parts(g + 1, g + 1) if g + 1 < NG else None

                        def w_du(ut=ut, ukey=ukey):
                            OP("dve", "tensor_tensor", [ukey], ["duT"], out=duT[:], in0=ut[:, :, 0:512], in1=ut[:, :, 1:513],
                               op=ALU.subtract)

                        def w_kt(p, g=g, ut=ut, ukey=ukey):
                            b = nb()
                            proj_fm(b, WA, "WA", 256 + p * 128, ut, ukey)
                            OP("act", "copy", [pk(b)], [("KT", g)], out=KT[:, p, g * 512:(g + 1) * 512], in_=PS[b][:])

                        def w_dv(tb, g=g):
                            b = nb()
                            proj_tm(b, duT, "duT", 0, tb, WA, "WA", 512, 256)
                            OP("dve", "tensor_copy", [pk(b)], [("DV", g)], out=DV[:, g * 4 + tb, :], in_=PS[b][:, 0:256])

                        def w_own(p, oj=oj, ut=ut, ukey=ukey):
                            b = nb()
                            proj_fm(b, WA, "WA", p * 128, ut, ukey)
                            OP("act", "copy", [pk(b)], [("QT", oj)], out=QT[:, p, oj, :], in_=PS[b][:])
                            b = nb()
                            proj_fm(b, WA, "WA", 512 + p * 128, ut, ukey, shift=1)
                            OP("dve", "tensor_copy", [pk(b)], [("VsT", oj)], out=VsT[:, p, oj, :], in_=PS[b][:])
                        quarters = [[w_du, lambda: w_kt(0), lambda: w_kt(1)],
                                    [lambda: w_dv(0), lambda: w_dv(1)],
                                    [lambda: w_dv(2), lambda: w_dv(3)],
                                    ([lambda: w_own(0), lambda: w_own(1)] if own else [])]
                        interleave(nxt, quarters)
                        cur = nxt

                    for oj, g in enumerate(OWN if 'S' not in SKIP else []):
                        nsteps = 4 * g + 4

                        def zmm(p, kb):
                            for i in range(2):
                                rows = slice(i * 64, (i + 1) * 64)
                                OP("pe", "matmul", [("KT", kb // 4), ("QT", oj)], [pk(2 * p + i)], PSZ[p][:, i, :],
                                   lhsT=KT[rows, p, kb * 128:(kb + 1) * 128], rhs=QT[rows, p, oj, :], start=True, stop=True)
                        for p in range(2):
                            zmm(p, 4 * g + 3)
                        for stp in range(nsteps):
                            kb = 4 * g + 3 - stp
                            diag = stp < 4
                            cp = 3 - stp
                            for p in range(2):
                                for i in range(2):
                                    OP("act", "activation", [pk(2 * p + i)], ["E%d%d" % (p, i)], out=Et[p][:, i, :],
                                       in_=PSZ[p][:, i, :], func=AF.Exp, scale=SB_SCALE)
                            for p in range(2):
                                for i in range(2):
                                    spk = "sp%d%d" % (p, i)
                                    OP("act", "activation", ["E%d%d" % (p, i)], [spk], out=spt[p][:, i, :], in_=Et[p][:, i, :],
                                       func=AF.Ln, bias=1.0)
                                    if diag:
                                        OP("dve", "tensor_tensor", [spk, "dmask"], [spk], out=spt[p][:, i, :],
                                           in0=spt[p][:, i, :], in1=dmask[:, cp, :], op=ALU.mult)
                            for p in range(2):
                                for i in range(2):
                                    OP("pe", "matmul", ["triK", "sp%d%d" % (p, i)], [pk(2 * p + i)], PSZ[p][:, i, :], lhsT=triK[:],
                                       rhs=spt[p][:, i, :], start=True, stop=(stp == 0))
                                    if stp > 0:
                                        OP("pe", "matmul", ["ones", "R%d" % p], [pk(2 * p + i)], PSZ[p][:, i, :], lhsT=ones[:],
                                           rhs=Rt[p][:, i, :], start=False, stop=True)
                            for p in range(2):
                                for i in range(2):
                                    pkk = "P%d%d" % (p, i)
                                    OP("act", "activation", [pk(2 * p + i)], [pkk], out=Pt[p][:, i, :], in_=PSZ[p][:, i, :],
                                       func=AF.Exp, scale=-1.0)
                                    if diag:
                                        OP("dve", "tensor_tensor", [pkk, "dmask"], [pkk], out=Pt[p][:, i, :],
                                           in0=Pt[p][:, i, :], in1=dmask[:, cp, :], op=ALU.mult)
                            for p in range(2):
                                if stp + 1 < nsteps:
                                    zmm(p, kb - 1)
                            for p in range(2):
                                for i in range(2):
                                    hcol = slice((p * 2 + i) * 64, (p * 2 + i + 1) * 64)
                                    OP("pe", "matmul", [("DV", kb // 4), "P%d%d" % (p, i)], [pk(4 + p)], PS[4 + p][i * 64:(i + 1) * 64, :],
                                       lhsT=DV[:, kb, hcol], rhs=Pt[p][:, i, :], start=(stp == 0), stop=(stp == nsteps - 1))
                            if stp + 1 < nsteps:
                                for p in range(2):
                                    if stp == 0:
                                        OP("dve", "tensor_copy", ["sp%d0" % p, "sp%d1" % p], ["R%d" % p], out=Rt[p][:], in_=spt[p][:])
                                    else:
                                        OP("dve", "tensor_tensor", ["sp%d0" % p, "sp%d1" % p, "R%d" % p], ["R%d" % p], out=Rt[p][:],
                                           in0=Rt[p][:], in1=spt[p][:], op=ALU.add)
                        for p in range(2):
                            pg = hp * 2 + p
                            OP("dve", "tensor_tensor", [pk(4 + p), ("VsT", oj)], [("yT", pg, oj)],
                               out=yT[:, pg, oj * 512:(oj + 1) * 512], in0=PS[4 + p][:], in1=VsT[:, p, oj, :], op=ALU.add)
                S.barrier_all()


        for _ph in os.environ.get('KORDER', 'AH'):
            (pass_A if _ph == 'A' else pass_H)()

        with ExitStack() as cs:
            WC = sb("WC", [128, 8, 3072], BF16, cs)
            load_win(WC, "WC", 0, C_SBG, 512)
            load_win(WC, "WC", 512, C_HGG, 512)
            load_win(WC, "WC", 1024, C_GSB, 1024)
            load_win(WC, "WC", 2048, C_GHG, 1024)
            Wso = sb("Wso", [128, 4, D], BF16, cs)
            Who = sb("Who", [128, 4, D], BF16, cs)
            Wo = sb("Wo", [128, 8, D], BF16, cs)
            for q in range(4):
                for q0 in range(0, 1024, WCH):
                    load_w(Wso[:, q, q0:q0 + WCH], "Wso", w_sbo, q * 128, q0, WCH)
                    load_w(Who[:, q, q0:q0 + WCH], "Who", w_hgo, q * 128, q0, WCH)
            for c in range(8):
                for q0 in range(0, 1024, WCH):
                    load_w(Wo[:, c, q0:q0 + WCH], "Wo", w_out, c * 128, q0, WCH)
            fng = cload("fng", [128, D], F32, fng_d, cs)
            ysbT = sb("ysbT", [128, 4, 512], BF16, cs)
            yhgT = sb("yhgT", [128, 4, 512], BF16, cs)
            yhg2 = [sb("yhg%d" % i, [128, 512], BF16, cs) for i in range(2)]
            mT = sb("mT", [128, 8, 512], BF16, cs)
            c1b = [sb("c1_%d" % i, [128, 512], F32, cs) for i in range(2)]
            c2b = [sb("c2_%d" % i, [128, 512], F32, cs) for i in range(2)]
            hnb = [sb("hn%d" % i, [128, D], F32, cs) for i in range(2)]
            stc = sb("stc", [128, 4], F32, cs)

            def sigmoid_from(bank, tmp, tkey):
                OP("act", "activation", [pk(bank)], [tkey], out=tmp[:], in_=PS[bank][:], func=AF.Exp, scale=-1.0)
                recip1p(tmp[:], tkey)

            OWNC = OWN if 'C' not in SKIP else []
            cur = token_group_parts(OWNC[0], 0) if OWNC else None
            if cur is not None:
                for p_ in cur[2]:
                    p_()
            for oj, g in enumerate(OWNC):
                ut, ukey = cur[0], cur[1]
                nxt = token_group_parts(OWNC[oj + 1], oj + 1) if oj + 1 < len(OWNC) else None
                def part_sb():
                    for pg in range(4):
                        b = pg % 2
                        c1, k1 = c1b[pg % 2], "c1_%d" % (pg % 2)
                        proj_fm(b, WC, "WC", pg * 128, ut, ukey)
                        sigmoid_from(b, c1, k1)
                        OP("dve", "tensor_tensor", [pk(b), k1], [k1], out=c1[:], in0=PS[b][:], in1=c1[:], op=ALU.mult)
                        OP("dve", "tensor_tensor", [k1, ("yT", pg, oj)], ["ysbT"], out=ysbT[:, pg, :], in0=c1[:],
                           in1=yT[:, pg, oj * 512:(oj + 1) * 512], op=ALU.mult)
                def part_hg():
                    for tb in range(4):
                        b = 2 + tb % 2
                        c2, k2 = c2b[tb % 2], "c2_%d" % (tb % 2)
                        yhg, ky = yhg2[tb % 2], "yhg%d" % (tb % 2)
                        proj_tm(b, ut, ukey, 1, tb, WC, "WC", 512, 512)
                        sigmoid_from(b, c2, k2)
                        OP("dve", "tensor_tensor", [pk(b), k2], [k2], out=c2[:], in0=PS[b][:], in1=c2[:], op=ALU.mult)
                        OP("dve", "tensor_tensor", [k2, "onall"], [ky], out=yhg[:], in0=c2[:],
                           in1=onall[:, oj * 4 + tb, :], op=ALU.mult)
                        for q in range(4):
                            OP("pe", "transpose", [ky, "ident"], ["ptb"], out=PTB[:, q, :], in_=yhg[:, q * 128:(q + 1) * 128],
                               identity=ident[:])
                        OP("act", "copy", ["ptb"], ["yhgT"], out=yhgT[:, :, tb * 128:(tb + 1) * 128], in_=PTB[:, 0:4, :])
                def w_cc(cc):
                    cs_ = slice(cc * 128, (cc + 1) * 128)
                    bs = [(4 * cc + k) % 7 for k in range(4)]
                    c1, k1 = c1b[cc % 2], "c1_%d" % (cc % 2)
                    c2, k2 = c2b[cc % 2], "c2_%d" % (cc % 2)
                    proj_fm(bs[0], WC, "WC", 1024 + cc * 128, ut, ukey)
                    proj_fm(bs[1], WC, "WC", 2048 + cc * 128, ut, ukey)
                    for q in range(4):
                        OP("pe", "matmul", ["Wso", "ysbT"], [pk(bs[2])], PS[bs[2]][:], lhsT=Wso[:, q, cs_], rhs=ysbT[:, q, :],
                           start=(q == 0), stop=(q == 3))
                    for q in range(4):
                        OP("pe", "matmul", ["Who", "yhgT"], [pk(bs[3])], PS[bs[3]][:], lhsT=Who[:, q, cs_], rhs=yhgT[:, q, :],
                           start=(q == 0), stop=(q == 3))
                    sigmoid_from(bs[0], c1, k1)
                    sigmoid_from(bs[1], c2, k2)
                    OP("dve", "tensor_tensor", [pk(bs[2]), k1], [k1], out=c1[:], in0=PS[bs[2]][:], in1=c1[:], op=ALU.mult)
                    OP("dve", "tensor_tensor", [pk(bs[3]), k2], [k2], out=c2[:], in0=PS[bs[3]][:], in1=c2[:], op=ALU.mult)
                    OP("dve", "tensor_tensor", [k1, k2], [("mT", cc)], out=mT[:, cc, :], in0=c1[:], in1=c2[:], op=ALU.add)
                def part_delta():
                    for tb in range(4):
                        o_i = tb % 2
                        hn = hnb[o_i]
                        hk = "hn%d" % o_i
                        xi = load_x(g * 512 + tb * 128)
                        for half in range(2):
                            bk = half
                            for c in range(8):
                                OP("pe", "matmul", [("mT", c), "Wo"], [pk(bk)], PS[bk][:], lhsT=mT[:, c, tb * 128:(tb + 1) * 128],
                                   rhs=Wo[:, c, half * 512:(half + 1) * 512], start=(c == 0), stop=(c == 7))
                            OP("dve", "tensor_tensor", [pk(bk), "xt%d" % xi], [hk], out=hn[:, half * 512:(half + 1) * 512],
                               in0=PS[bk][:], in1=xt[xi][:, half * 512:(half + 1) * 512], op=ALU.add)
                        rstd_chain(hn[:], hk, stc, "stc")
                        OP("dve", "scalar_tensor_tensor", [hk, "stc", "fng"], [hk], out=hn[:], in0=hn[:],
                           scalar=stc[:, 3:4], in1=fng[:], op0=ALU.mult, op1=ALU.mult)
                        row0 = oj * 512 + tb * 128
                        S.dma("out%d" % o_i, [hk], [], out_d[row0:row0 + 128, :], hn[:])
                interleave(nxt, [[part_sb], [part_hg], [lambda: [w_cc(c_) for c_ in range(4)]],
                                 [lambda: [w_cc(c_) for c_ in range(4, 8)], part_delta]])
                cur = nxt
            S.barrier_all()
        S.emit()
        es_mem.pop_all()
    return nc


NG_FULL = 17
OWN_FULL = [4, 8, 12, 16]


def make_in_maps(x, meta, norm_g, w_in, w_sb_out, w_hg_out, w_out, hg_norm_g, hg_lb_logits, final_norm_g,
                 NG=NG_FULL, OWN=OWN_FULL, ncores_per_batch=4):
    B = x.shape[0]
    consts = host_consts()
    f32 = np.float32
    shared = {
        "w_in": np.ascontiguousarray(w_in[0], f32),
        "w_sb_out": np.ascontiguousarray(w_sb_out[0], f32),
        "w_hg_out": np.ascontiguousarray(w_hg_out[0], f32),
        "w_out": np.ascontiguousarray(w_out[0], f32),
        "ngT": np.ascontiguousarray(norm_g[0].reshape(8, 128).T, f32),
        "hgn_bc": np.ascontiguousarray(np.broadcast_to(hg_norm_g[0][None, :], (128, 512)), f32),
        "fng_bc": np.ascontiguousarray(np.broadcast_to(final_norm_g[None, :], (128, D)), f32),
        "lbl_bc": np.ascontiguousarray(np.broadcast_to(hg_lb_logits[None, :, :], (128, 2, 512)), f32),
        "lblT": np.ascontiguousarray(hg_lb_logits.reshape(2, 4, 128).transpose(2, 0, 1), f32),
    }
    shared.update(consts)
    in_maps = []
    nmeta = meta.shape[0]
    for b in range(B):
        G = np.concatenate([np.zeros((512 - nmeta, D), f32), meta.astype(f32), x[b].astype(f32)], axis=0)
        for r in range(ncores_per_batch):
            nz = (ncores_per_batch - 1 - r) * 512
            nreal = NG * 512 - nz
            xl = np.concatenate([np.zeros((nz, D), f32), G[:nreal]], axis=0)
            m = dict(shared)
            m["xloc"] = np.ascontiguousarray(xl)
            in_maps.append(m)
    return in_maps


_NC_CACHE = {}


def kernel(x, meta, norm_g, w_in, w_sb_out, w_hg_out, w_out, hg_norm_g, hg_lb_logits, final_norm_g):
    x = np.asarray(x)
    B, SEQ, _ = x.shape
    in_maps = make_in_maps(x, np.asarray(meta), np.asarray(norm_g), np.asarray(w_in), np.asarray(w_sb_out),
                           np.asarray(w_hg_out), np.asarray(w_out), np.asarray(hg_norm_g),
                           np.asarray(hg_lb_logits), np.asarray(final_norm_g))
    key = (NG_FULL, tuple(OWN_FULL))
    if key not in _NC_CACHE:
        _NC_CACHE[key] = build(NG_FULL, OWN_FULL)
    nc = _NC_CACHE[key]
    res = run_bass_kernel_spmd(nc, in_maps, core_ids=list(range(len(in_maps))))
    out = np.empty((B, SEQ, D), np.float32)
    for b in range(B):
        for r in range(4):
            o = np.asarray(res.results[b * 4 + r]["out"])
            for j in range(4):
                row = (4 * j + r) * 512
                out[b, row:row + 512] = o[j * 512:(j + 1) * 512]
    return out
```

```python
import os
import numpy as np
import ml_dtypes
from contextlib import ExitStack
import concourse.bass as bass
import concourse.mybir as mybir
from concourse.bass_utils import run_bass_kernel_spmd

F32 = mybir.dt.float32
BF16 = mybir.dt.bfloat16
AF = mybir.ActivationFunctionType
ALU = mybir.AluOpType
AX = mybir.AxisListType

D = 1024
EPS = 1e-6
SB_SCALE = 64 ** -0.5
HG_SCALE = 128 ** -0.5
C_SBQ, C_SBK, C_SBV, C_SBG, C_HGQ, C_HGF, C_HGI, C_HGG, C_GSB, C_GHG = (
    0, 512, 1024, 1536, 2048, 2560, 3072, 3584, 4096, 5120)


class Sync:
    ENG = ("pe", "act", "dve", "pool", "sp")

    def __init__(self, nc, es):
        self.nc = nc
        self.es = es
        self.sem = {e: es.enter_context(nc.semaphore("s_" + e)) for e in self.ENG}
        self.cnt = {e: 0 for e in self.ENG}
        self.prog = {e: [] for e in self.ENG}
        self.waited = {e: {} for e in self.ENG}
        self.bufs = {}
        self.dma_sems = {}
        self.alias = {}

    def _expand(self, keys):
        out = []
        for k in keys:
            out.extend(self.alias.get(k, (k,)))
        return out

    def _buf(self, k):
        b = self.bufs.get(k)
        if b is None:
            b = self.bufs[k] = ({}, {})
        return b

    def _deps(self, eng, reads, writes):
        deps = {}

        def add(d):
            for s, (sem, v) in d.items():
                if deps.get(s, (None, -1))[1] < v:
                    deps[s] = (sem, v)
        for k in reads:
            add(self._buf(k)[0])
        for k in writes:
            w, r = self._buf(k)
            add(w)
            add(r)
        for s, (sem, v) in deps.items():
            if eng == "pe" and s == "pe":
                continue
            if self.waited[eng].get(s, -1) >= v:
                continue
            self.waited[eng][s] = v
            self.prog[eng].append(("wait", sem, v))

    def _commit(self, tokname, tok, reads, writes):
        for k in reads:
            self._buf(k)[1][tokname] = tok
        for k in writes:
            w, r = self._buf(k)
            w[tokname] = tok
            r.clear()

    def op(self, eng, meth, r, w, *a, **k):
        r = self._expand(r)
        self._deps(eng, r, w)
        self.cnt[eng] += 1
        sem = self.sem[eng]
        self.prog[eng].append(("op", (meth, a, k), sem, 1))
        self._commit(eng, (sem, self.cnt[eng]), r, w)

    def dma(self, slot, r, w, out, in_, eng="sp"):
        if slot not in self.dma_sems:
            self.dma_sems[slot] = [self.es.enter_context(self.nc.semaphore("d_" + slot)), 0]
        self._deps(eng, r, w)
        ds = self.dma_sems[slot]
        ds[1] += 16
        self.prog[eng].append(("op", ("dma_start", (), dict(out=out, in_=in_)), ds[0], 16))
        self._commit("dma_" + slot, (ds[0], ds[1]), r, w)

    def barrier_all(self):
        toks = {e: (self.sem[e], self.cnt[e]) for e in self.ENG if self.cnt[e] > 0}
        for s, ds in self.dma_sems.items():
            toks["dma_" + s] = (ds[0], ds[1])
        for e in self.ENG:
            for s, (sem, v) in toks.items():
                if s == e or v == 0:
                    continue
                if self.waited[e].get(s, -1) >= v:
                    continue
                self.waited[e][s] = v
                self.prog[e].append(("wait", sem, v))
        self.bufs = {}

    def emit(self):
        nc = self.nc
        emap = {"pe": "tensor", "act": "scalar", "dve": "vector", "pool": "gpsimd", "sp": "sync"}
        with nc.Block() as block:
            for e in self.ENG:
                prog = self.prog[e]

                def body(engine, prog=prog):
                    for it in prog:
                        if it[0] == "wait":
                            engine.wait_ge(it[1], it[2])
                        else:
                            meth, a, k = it[1]
                            getattr(engine, meth)(*a, **k).then_inc(it[2], it[3])
                getattr(block, emap[e])(body)


def host_consts():
    s = np.arange(128)[:, None]
    t = np.arange(128)[None, :]
    same = (s // 64) == (t // 64)
    c = {}
    c["ident"] = np.eye(128).astype(ml_dtypes.bfloat16)
    c["triC"] = (same & (s <= t)).astype(np.float32)
    c["triSU"] = (same & (s > t)).astype(np.float32)
    c["triK"] = (s >= t).astype(ml_dtypes.bfloat16)
    c["ones"] = np.ones((128, 128), ml_dtypes.bfloat16)
    dm = np.zeros((128, 4, 4, 128), np.float32)
    for cp in range(4):
        for cq in range(4):
            if cp < cq:
                dm[:, cp, cq, :] = 1.0
            elif cp == cq:
                dm[:, cp, cq, :] = (s < t)
    c["dmask"] = dm.reshape(128, 4, 512).astype(ml_dtypes.bfloat16)
    return c


def build(NG, OWN):
    NOWN = len(OWN)
    T = NG * 512
    NB = NG * 4
    nc = bass.Bass("TRN2", target_bir_lowering=False)

    def din(name, shape, dt=F32):
        return nc.dram_tensor(name, list(shape), dt, kind="ExternalInput").ap()
    xloc = din("xloc", [T, D])
    w_in = din("w_in", [D, 6144])
    w_sbo = din("w_sb_out", [512, D])
    w_hgo = din("w_hg_out", [512, D])
    w_out = din("w_out", [D, D])
    ngT_d = din("ngT", [128, 8])
    hgn_d = din("hgn_bc", [128, 512])
    fng_d = din("fng_bc", [128, D])
    lbl_d = din("lbl_bc", [128, 2, 512])
    lblT_d = din("lblT", [128, 2, 4])
    ident_d = din("ident", [128, 128], BF16)
    triC_d = din("triC", [128, 128])
    triSU_d = din("triSU", [128, 128])
    triK_d = din("triK", [128, 128], BF16)
    ones_d = din("ones", [128, 128], BF16)
    dmask_d = din("dmask", [128, 4, 512], BF16)
    out_d = nc.dram_tensor("out", [NOWN * 512, D], F32, kind="ExternalOutput").ap()

    with ExitStack() as es:
        S = Sync(nc, es)
        OP = S.op

        es_mem = ExitStack()

        def sb(name, shape, dt, scope=es_mem):
            return scope.enter_context(nc.sbuf_tensor(name, list(shape), dt))

        PS01 = es_mem.enter_context(nc.psum_tensor("ps01", [128, 2, 512], F32))
        PS23 = es_mem.enter_context(nc.psum_tensor("ps23", [128, 2, 512], F32))
        PS = [PS01[:, 0, :], PS01[:, 1, :], PS23[:, 0, :], PS23[:, 1, :]]
        PS += [es_mem.enter_context(nc.psum_tensor("ps%d" % i, [128, 512], F32))[:] for i in range(4, 7)]
        PTB = es_mem.enter_context(nc.psum_tensor("ptb", [128, 8, 128], BF16))
        PSZ = [PS01, PS23]

        def pk(i):
            return ("ps", i)

        def cload(name, shape, dt, src, scope=None):
            scope = es_mem if scope is None else scope
            tl = sb(name + "_s", shape, dt, scope)
            S.dma("c_" + name, [], [name], tl[:], src)
            return tl

        ident = cload("ident", [128, 128], BF16, ident_d)
        triC = cload("triC", [128, 128], F32, triC_d)
        triSU = cload("triSU", [128, 128], F32, triSU_d)
        triK = cload("triK", [128, 128], BF16, triK_d)
        ones = cload("ones", [128, 128], BF16, ones_d)
        dmask = cload("dmask", [128, 4, 512], BF16, dmask_d)
        ngT = cload("ngT", [128, 8], F32, ngT_d)

        yT = sb("yT", [128, 4, NOWN * 512], BF16)
        onall = sb("onall", [128, NOWN * 4, 512], BF16)

        NXS = 3
        xt = [sb("xt%d" % i, [128, D], F32) for i in range(NXS)]
        junkd = sb("junkd", [128, D], BF16)
        xs = [sb("xs%d" % i, [128, D], BF16) for i in range(2)]
        uT = [sb("uT%d" % i, [128, 8, 513], BF16) for i in range(2)]
        st = [sb("st%d" % i, [128, 4], F32) for i in range(NXS)]
        NWS = 4
        WCH = 512
        wst = [sb("wst%d" % i, [128, WCH], F32) for i in range(NWS)]
        wst_i = [0]
        x_i = [0]

        def load_w(dst, dkey, src, row0, col0, ncols, gcol=None):
            i = wst_i[0] % NWS
            on_act = (wst_i[0] % 2) == 1
            wst_i[0] += 1
            if dkey not in S.alias:
                S.alias[dkey] = [(dkey, 0), (dkey, 1)]
            dkey = (dkey, 1 if on_act else 0)
            S.dma("wst%d" % i, [], ["wst%d" % i], wst[i][:, 0:ncols], src[row0:row0 + 128, col0:col0 + ncols])
            if gcol is None:
                if on_act:
                    OP("act", "copy", ["wst%d" % i], [dkey], out=dst, in_=wst[i][:, 0:ncols])
                else:
                    OP("dve", "tensor_copy", ["wst%d" % i], [dkey], out=dst, in_=wst[i][:, 0:ncols])
            else:
                if on_act:
                    OP("act", "activation", ["wst%d" % i, "ngT"], [dkey], out=dst, in_=wst[i][:, 0:ncols],
                       func=AF.Copy, scale=gcol)
                else:
                    OP("dve", "tensor_scalar", ["wst%d" % i, "ngT"], [dkey], out=dst, in0=wst[i][:, 0:ncols],
                       scalar1=gcol, scalar2=None, op0=ALU.mult)

        def load_win(Wt, wkey, off, col0, ncols):
            for c in range(8):
                for q0 in range(0, ncols, WCH):
                    n = min(WCH, ncols - q0)
                    load_w(Wt[:, c, off + q0:off + q0 + n], wkey, w_in, c * 128, col0 + q0, n, gcol=ngT[:, c:c + 1])

        def rstd_chain(src, srckey, stt, sttkey, n=D):
            OP("dve", "scalar_tensor_tensor", [srckey], [sttkey], out=junkd[:, 0:n], in0=src, scalar=1.0, in1=src,
               op0=ALU.mult, op1=ALU.mult, accum_out=stt[:, 0:1])
            OP("dve", "tensor_scalar", [sttkey], [sttkey], out=stt[:, 1:2], in0=stt[:, 0:1], scalar1=1.0 / n,
               scalar2=EPS, op0=ALU.mult, op1=ALU.add)
            OP("act", "activation", [sttkey], [sttkey], out=stt[:, 2:3], in_=stt[:, 1:2], func=AF.Ln)
            OP("act", "activation", [sttkey], [sttkey], out=stt[:, 3:4], in_=stt[:, 2:3], func=AF.Exp, scale=-0.5)

        def load_x(row0):
            i = x_i[0] % NXS
            x_i[0] += 1
            S.dma("x%d" % i, [], ["xt%d" % i], xt[i][:], xloc[row0:row0 + 128, :])
            return i

        def token_group_parts(g, gi, evac="act"):
            ub = gi % 2
            ut = uT[ub]
            ukey = "uT%d" % ub
            slot = {}

            def st_load(tb):
                slot[tb] = load_x(g * 512 + tb * 128)

            def st_stats(tb):
                i = slot[tb]
                rstd_chain(xt[i][:], "xt%d" % i, st[i], "st%d" % i)

            def st_rest(tb):
                i = slot[tb]
                j = tb % 2
                OP("dve", "tensor_scalar", ["xt%d" % i, "st%d" % i], ["xs%d" % j], out=xs[j][:], in0=xt[i][:],
                   scalar1=st[i][:, 3:4], scalar2=None, op0=ALU.mult)
                for c in range(8):
                    OP("pe", "transpose", ["xs%d" % j, "ident"], ["ptb"], out=PTB[:, c, :],
                       in_=xs[j][:, c * 128:(c + 1) * 128], identity=ident[:])
                if evac == "act":
                    OP("act", "copy", ["ptb"], [ukey], out=ut[:, :, 1 + tb * 128:1 + (tb + 1) * 128], in_=PTB[:])
                else:
                    OP("dve", "tensor_copy", ["ptb"], [ukey], out=ut[:, :, 1 + tb * 128:1 + (tb + 1) * 128], in_=PTB[:])
            def part(tb):
                if tb == 0:
                    st_load(0)
                    st_load(1)
                    st_stats(0)
                if tb + 2 < 4:
                    st_load(tb + 2)
                if tb + 1 < 4:
                    st_stats(tb + 1)
                st_rest(tb)
                if tb == 3:
                    if gi == 0:
                        OP("pool", "memset", [], [ukey], ut[:, :, 0:1], 0.0)
                    else:
                        pv = uT[(gi - 1) % 2]
                        OP("pool", "tensor_copy", ["uT%d" % ((gi - 1) % 2)], [ukey], out=ut[:, :, 0:1],
                           in_=pv[:, :, 512:513])
            return ut, ukey, [(lambda tb=tb: part(tb)) for tb in range(4)]

        def token_group(g, gi):
            ut, ukey, parts = token_group_parts(g, gi)
            for p_ in parts:
                p_()
            return ut, ukey

        def interleave(nxt, quarters):
            for q in range(4):
                if nxt is not None:
                    nxt[2][q]()
                for f in quarters[q]:
                    f()

        def proj_fm(bank, Wt, wkey, c0, ut, ukey, shift=0, ncols=128):
            for c in range(8):
                OP("pe", "matmul", [wkey, ukey], [pk(bank)], PS[bank][0:ncols, :], lhsT=Wt[:, c, c0:c0 + ncols],
                   rhs=ut[:, c, 1 - shift:513 - shift], start=(c == 0), stop=(c == 7))

        def proj_tm(bank, src, skey, off, tb, Wt, wkey, c0, ncols):
            for c in range(8):
                OP("pe", "matmul", [wkey, skey], [pk(bank)], PS[bank][:, 0:ncols],
                   lhsT=src[:, c, off + tb * 128:off + (tb + 1) * 128], rhs=Wt[:, c, c0:c0 + ncols],
                   start=(c == 0), stop=(c == 7))

        def recip1p(tl, key):
            OP("act", "activation", [key], [key], out=tl, in_=tl, func=AF.Ln, bias=1.0)
            OP("act", "activation", [key], [key], out=tl, in_=tl, func=AF.Exp, scale=-1.0)

        SKIP = os.environ.get('KSKIP', '')
        def pass_H():
            with ExitStack() as hs:
                WH = sb("WH", [128, 8, 1536], BF16, hs)
                load_win(WH, "WH", 0, C_HGQ, 1536)
                hgn = cload("hgn", [128, 512], F32, hgn_d, hs)
                lbl = cload("lbl", [128, 2, 512], F32, lbl_d, hs)
                lblT = cload("lblT", [128, 2, 4], F32, lblT_d, hs)
                oml = sb("oml", [128, 512], F32, hs)
                omlT = sb("omlT", [128, 4], F32, hs)
                for (dst, dk, src, sk) in ((oml, "oml", lbl, "lbl"), (omlT, "omlT", lblT, "lblT")):
                    OP("dve", "tensor_tensor", [sk], [dk], out=dst[:], in0=src[:, 0, :], in1=src[:, 1, :], op=ALU.subtract)
                    OP("act", "activation", [dk], [dk], out=dst[:], in_=dst[:], func=AF.Exp)
                    recip1p(dst[:], dk)

                tA = [sb("h_tA%d" % i, [128, 512], F32, hs) for i in range(4)]
                tB = [sb("h_tB%d" % i, [128, 512], F32, hs) for i in range(4)]
                tC = [sb("h_tC%d" % i, [128, 512], F32, hs) for i in range(4)]
                tD = [sb("h_tD%d" % i, [128, 512], F32, hs) for i in range(4)]
                kk = [sb("h_kk%d" % i, [128, 512], F32, hs) for i in range(4)]
                gtok = sb("h_g", [128, 4, 512], F32, hs)
                kdl2 = [sb("h_kdl%d" % i, [128, 4, 512], BF16, hs) for i in range(2)]
                vc2 = [sb("h_vc%d" % i, [128, 4, 512], BF16, hs) for i in range(2)]
                kdT2 = [sb("h_kdT%d" % i, [128, 4, 512], BF16, hs) for i in range(2)]
                dec2 = [sb("h_dec%d" % i, [128, 4, 8], F32, hs) for i in range(2)]
                qz = sb("h_qz", [128, 4, 4, 2, 128], BF16, hs)
                Sst = sb("h_S", [128, 4, 128], F32, hs)
                Sbf = sb("h_Sbf", [128, 2, 4, 2, 128], BF16, hs)
                atm4 = [sb("h_atm%d" % i, [128, 4, 128], BF16, hs) for i in range(2)]
                osb4 = [sb("h_o%d" % i, [128, 512], F32, hs) for i in range(2)]
                ost4 = [sb("h_ost%d" % i, [128, 16], F32, hs) for i in range(2)]
                triC4 = sb("h_triC4", [128, 4, 128], F32, hs)
                for h in range(4):
                    OP("pool", "tensor_copy", ["triC"], ["triC4"], out=triC4[:, h, :], in_=triC[:])
                OP("pool", "memset", [], [("qz", h) for h in range(4)], qz[:], 0.0)
                OP("pool", "memset", [], [("S", h) for h in range(4)], Sst[:], 0.0)

                cur = token_group_parts(0, 0)
                if 'H' not in SKIP:
                    for p_ in cur[2]:
                        p_()
                for g in range(NG if 'H' not in SKIP else 0):
                    own = g in OWN
                    oj = OWN.index(g) if own else -1
                    ut, ukey = cur[0], cur[1]
                    nxt = token_group_parts(g + 1, g + 1) if g + 1 < NG else None
                    gp = g % 2
                    kdl, vc, kdT, dec = kdl2[gp], vc2[gp], kdT2[gp], dec2[gp]
                    K_vc = lambda tb: ("vc", gp, tb)
                    K_kdl = lambda tb: ("kdl", gp, tb)
                    K_kdT = lambda h: ("kdT", gp, h)
                    K_dec = lambda h: ("dec", gp, h)
                    def part_T1():
                        for tb in range(4):
                            proj_tm(0, ut, ukey, 1, tb, WH, "WH", 512, 512)
                            proj_tm(1, ut, ukey, 1, tb, WH, "WH", 1024, 512)
                            ka = "tA%d" % tb
                            OP("act", "activation", [pk(0)], [ka], out=tA[tb][:], in_=PS[0][:], func=AF.Exp)
                            OP("act", "copy", [pk(1)], [K_vc(tb)], out=vc[:, tb, :], in_=PS[1][:])
                            recip1p(tA[tb][:], ka)
                            OP("dve", "tensor_tensor", [ka, "oml"], ["kk%d" % tb], out=kk[tb][:], in0=tA[tb][:], in1=oml[:],
                               op=ALU.mult)
                    def part_T23():
                        for tb in range(4):
                            OP("act", "activation", ["kk%d" % tb], [("gtok", tb)], out=gtok[:, tb, :], in_=kk[tb][:], func=AF.Ln,
                               scale=-1.0, bias=1.0)
                        for tb in range(4):
                            OP("pe", "matmul", ["triSU", ("gtok", tb)], [pk(2)], PS[2][:], lhsT=triSU[:], rhs=gtok[:, tb, :],
                               start=True, stop=True)
                            OP("act", "activation", [pk(2)], ["tB%d" % tb], out=tB[tb][:], in_=PS[2][:], func=AF.Exp)
                            OP("dve", "tensor_tensor", ["kk%d" % tb, "tB%d" % tb], [K_kdl(tb)], out=kdl[:, tb, :], in0=kk[tb][:],
                               in1=tB[tb][:], op=ALU.mult)
                            for h in range(4):
                                OP("pe", "matmul", ["triC", ("gtok", tb)], [pk(3 + h)], PS[3 + h][:, tb * 128:(tb + 1) * 128],
                                   lhsT=gtok[:, tb, h * 128:(h + 1) * 128], rhs=triC[:], start=True, stop=True)
                    def part_F():
                        for h in range(4):
                            proj_fm(0, WH, "WH", 512 + h * 128, ut, ukey)
                            ka, kb_ = "tA%d" % h, "tB%d" % h
                            OP("act", "activation", [pk(0)], [ka], out=tA[h][:], in_=PS[0][:], func=AF.Exp)
                            recip1p(tA[h][:], ka)
                            OP("act", "activation", [pk(3 + h)], [kb_], out=tB[h][:], in_=PS[3 + h][:], func=AF.Exp, scale=-1.0)
                            OP("dve", "scalar_tensor_tensor", [ka, kb_, "omlT"], [K_kdT(h)], out=kdT[:, h, :], in0=tA[h][:],
                               scalar=omlT[:, h:h + 1], in1=tB[h][:], op0=ALU.mult, op1=ALU.mult)
                            OP("act", "activation", [pk(3 + h)], [K_dec(h)], out=dec[:, h, :],
                               in_=PS[3 + h][:].rearrange("p (c t) -> p c t", t=64)[:, :, 63], func=AF.Exp)
                        if own:
                            for h in range(4):
                                qb = h % 2
                                kc, kd = "tC%d" % h, "tD%d" % h
                                proj_fm(qb, WH, "WH", h * 128, ut, ukey)
                                OP("act", "activation", [pk(qb)], [kc], out=tC[h][:], in_=PS[qb][:], func=AF.Exp, scale=-1.0)
                                recip1p(tC[h][:], kc)
                                OP("dve", "tensor_tensor", [pk(qb), kc], [kc], out=tC[h][:], in0=PS[qb][:], in1=tC[h][:],
                                   op=ALU.mult)
                                OP("act", "activation", [pk(3 + h)], [kd], out=tD[h][:], in_=PS[3 + h][:], func=AF.Exp)
                                for ci in range(2):
                                    v4c = tC[h][:].rearrange("p (b c t) -> p b c t", b=4, c=2)[:, :, ci, :]
                                    v4d = tD[h][:].rearrange("p (b c t) -> p b c t", b=4, c=2)[:, :, ci, :]
                                    OP("dve", "scalar_tensor_tensor", [kc, kd], [("qz", h)],
                                       out=qz[:, h, :, ci, ci * 64:(ci + 1) * 64], in0=v4c, scalar=HG_SCALE, in1=v4d,
                                       op0=ALU.mult, op1=ALU.mult)
                    def part_R():
                        for cch in range(8):
                            tb, ci = cch // 2, cch % 2
                            for h in range(4):
                                hc = slice(h * 128, (h + 1) * 128)
                                rows = slice(ci * 64, (ci + 1) * 64)
                                if own:
                                    OP("pool", "tensor_copy", [("S", h)], [("Sbf", h, tb % 2)], out=Sbf[:, tb % 2, h, ci, :],
                                       in_=Sst[:, h, :])
                                OP("pe", "matmul", [K_kdl(tb), K_vc(tb)], [pk(3 + h)], PS[3 + h][:, 0:128], lhsT=kdl[rows, tb, hc],
                                   rhs=vc[rows, tb, hc], start=True, stop=True)
                                OP("dve", "scalar_tensor_tensor", [("S", h), K_dec(h), pk(3 + h)], [("S", h)], out=Sst[:, h, :],
                                   in0=Sst[:, h, :], scalar=dec[:, h, cch:cch + 1], in1=PS[3 + h][:, 0:128],
                                   op0=ALU.mult, op1=ALU.add)
                            if own and ci == 1:
                                blk = slice(tb * 128, (tb + 1) * 128)
                                pb = tb % 2
                                atm_, osb_, ost_ = atm4[pb], osb4[pb], ost4[pb]
                                ak, okk, sk = "atm%d" % pb, "osb%d" % pb, "ost%d" % pb
                                for h in range(4):
                                    reg = PS[2][:, h * 128:(h + 1) * 128]
                                    OP("pe", "matmul", [K_kdT(h), ("qz", h)], [pk(2)], reg, lhsT=kdT[:, h, blk],
                                       rhs=qz[:, h, tb, 0, :], start=True, stop=False)
                                    OP("pe", "matmul", [K_kdT(h), ("qz", h)], [pk(2)], reg, lhsT=kdT[:, h, blk],
                                       rhs=qz[:, h, tb, 1, :], start=False, stop=True)
                                OP("dve", "tensor_tensor", [pk(2), "triC4"], [ak], out=atm_[:],
                                   in0=PS[2][:].rearrange("p (h t) -> p h t", h=4), in1=triC4[:], op=ALU.mult)
                                for h in range(4):
                                    hc = slice(h * 128, (h + 1) * 128)
                                    reg = PS[1][:, hc]
                                    OP("pe", "matmul", [ak, K_vc(tb)], [pk(1)], reg, lhsT=atm_[:, h, :], rhs=vc[:, tb, hc],
                                       start=True, stop=False)
                                    OP("pe", "matmul", [("qz", h), ("Sbf", h, pb)], [pk(1)], reg, lhsT=qz[:, h, tb, 0, :],
                                       rhs=Sbf[:, pb, h, 0, :], start=False, stop=False)
                                    OP("pe", "matmul", [("qz", h), ("Sbf", h, pb)], [pk(1)], reg, lhsT=qz[:, h, tb, 1, :],
                                       rhs=Sbf[:, pb, h, 1, :], start=False, stop=True)
                                OP("act", "copy", [pk(1)], [okk], out=osb_[:], in_=PS[1][:])
                                for h in range(4):
                                    hc = slice(h * 128, (h + 1) * 128)
                                    OP("dve", "scalar_tensor_tensor", [okk], [sk], out=junkd[:, 0:128], in0=osb_[:, hc],
                                       scalar=1.0, in1=osb_[:, hc], op0=ALU.mult, op1=ALU.mult, accum_out=ost_[:, h:h + 1])
                                OP("dve", "tensor_scalar", [sk], [sk], out=ost_[:, 4:8], in0=ost_[:, 0:4], scalar1=1.0 / 128,
                                   scalar2=EPS, op0=ALU.mult, op1=ALU.add)
                                OP("act", "activation", [sk], [sk], out=ost_[:, 8:12], in_=ost_[:, 4:8], func=AF.Ln)
                                OP("act", "activation", [sk], [sk], out=ost_[:, 12:16], in_=ost_[:, 8:12], func=AF.Exp,
                                   scale=-0.5)
                                for h in range(4):
                                    hc = slice(h * 128, (h + 1) * 128)
                                    OP("dve", "scalar_tensor_tensor", [okk, sk, "hgn"], ["onall"],
                                       out=onall[:, oj * 4 + tb, hc], in0=osb_[:, hc], scalar=ost_[:, 12 + h:13 + h],
                                       in1=hgn[:, hc], op0=ALU.mult, op1=ALU.mult)
                    interleave(nxt, [[part_T1], [part_T23], [part_F], [part_R]])
                    cur = nxt
                S.barrier_all()


        def pass_A():
            with ExitStack() as as_:
                WA = sb("WA", [128, 8, 768], BF16, as_)
                KT = sb("KT", [128, 2, T], BF16, as_)
                DV = sb("DV", [128, NB, 256], BF16, as_)
                QT = sb("QT", [128, 2, NOWN, 512], BF16, as_)
                VsT = sb("VsT", [128, 2, NOWN, 512], BF16, as_)
                duT = sb("duT", [128, 8, 512], BF16, as_)
                Et = [sb("E%d" % p, [128, 2, 512], F32, as_) for p in range(2)]
                spt = [sb("sp%d" % p, [128, 2, 512], BF16, as_) for p in range(2)]
                Pt = [sb("P%d" % p, [128, 2, 512], BF16, as_) for p in range(2)]
                Rt = [sb("R%d" % p, [128, 2, 512], BF16, as_) for p in range(2)]

                for hp in range(2 if 'A' not in SKIP else 0):
                    h0 = hp * 4
                    load_win(WA, "WA", 0, C_SBQ + h0 * 64, 256)
                    load_win(WA, "WA", 256, C_SBK + h0 * 64, 256)
                    load_win(WA, "WA", 512, C_SBV + h0 * 64, 256)
                    bank_i = [0]
                    sweeping = [False]

                    def nb():
                        if sweeping[0]:
                            return 6
                        bank_i[0] = (bank_i[0] + 1) % 5
                        return (0, 1, 2, 3, 6)[bank_i[0]]

                    tg = [token_group_parts(g, g, evac="dve") for g in range(NG)]
                    items = list(tg[0][2])
                    end_idx = {}
                    for g in range(NG):
                        own = g in OWN
                        oj = OWN.index(g) if own else -1
                        ut, ukey = tg[g][0], tg[g][1]

                        def w_du(ut=ut, ukey=ukey):
                            OP("dve", "tensor_tensor", [ukey], ["duT"], out=duT[:], in0=ut[:, :, 0:512], in1=ut[:, :, 1:513],
                               op=ALU.subtract)

                        def w_kt(p, g=g, ut=ut, ukey=ukey):
                            b = nb()
                            proj_fm(b, WA, "WA", 256 + p * 128, ut, ukey)
                            OP("dve", "tensor_copy", [pk(b)], [("KT", g)], out=KT[:, p, g * 512:(g + 1) * 512], in_=PS[b][:])

                        def w_dv(tb, g=g):
                            b = nb()
                            proj_tm(b, duT, "duT", 0, tb, WA, "WA", 512, 256)
                            OP("dve", "tensor_copy", [pk(b)], [("DV", g)], out=DV[:, g * 4 + tb, :], in_=PS[b][:, 0:256])

                        def w_own(p, oj=oj, ut=ut, ukey=ukey):
                            b = nb()
                            proj_fm(b, WA, "WA", p * 128, ut, ukey)
                            OP("dve", "tensor_copy", [pk(b)], [("QT", oj)], out=QT[:, p, oj, :], in_=PS[b][:])
                            b = nb()
                            proj_fm(b, WA, "WA", 512 + p * 128, ut, ukey, shift=1)
                            OP("dve", "tensor_copy", [pk(b)], [("VsT", oj)], out=VsT[:, p, oj, :], in_=PS[b][:])
                        quarters = [[w_du, (lambda f=w_kt: f(0)), (lambda f=w_kt: f(1))],
                                    [(lambda f=w_dv: f(0)), (lambda f=w_dv: f(1))],
                                    [(lambda f=w_dv: f(2)), (lambda f=w_dv: f(3))],
                                    ([(lambda f=w_own: f(0)), (lambda f=w_own: f(1))] if own else [])]
                        for q in range(4):
                            fs = ([tg[g + 1][2][q]] if g + 1 < NG else []) + quarters[q]
                            items.append(lambda fs=fs: [f() for f in fs])
                        end_idx[g] = len(items)

                    def zmm(oj, p, kb):
                        for i in range(2):
                            rows = slice(i * 64, (i + 1) * 64)
                            OP("pe", "matmul", [("KT", kb // 4), ("QT", oj)], [pk(2 * p + i)], PSZ[p][:, i, :],
                               lhsT=KT[rows, p, kb * 128:(kb + 1) * 128], rhs=QT[rows, p, oj, :], start=True, stop=True)

                    def sweep_step(oj, g, stp, nsteps):
                        kb = 4 * g + 3 - stp
                        diag = stp < 4
                        cp = 3 - stp
                        for p in range(2):
                            for i in range(2):
                                OP("act", "activation", [pk(2 * p + i)], ["E%d%d" % (p, i)], out=Et[p][:, i, :],
                                   in_=PSZ[p][:, i, :], func=AF.Exp, scale=SB_SCALE)
                        for p in range(2):
                            for i in range(2):
                                spk = "sp%d%d" % (p, i)
                                OP("act", "activation", ["E%d%d" % (p, i)], [spk], out=spt[p][:, i, :], in_=Et[p][:, i, :],
                                   func=AF.Ln, bias=1.0)
                                if diag:
                                    OP("dve", "tensor_tensor", [spk, "dmask"], [spk], out=spt[p][:, i, :],
                                       in0=spt[p][:, i, :], in1=dmask[:, cp, :], op=ALU.mult)
                        for p in range(2):
                            for i in range(2):
                                OP("pe", "matmul", ["triK", "sp%d%d" % (p, i)], [pk(2 * p + i)], PSZ[p][:, i, :], lhsT=triK[:],
                                   rhs=spt[p][:, i, :], start=True, stop=(stp == 0))
                                if stp > 0:
                                    OP("pe", "matmul", ["ones", "R%d" % p], [pk(2 * p + i)], PSZ[p][:, i, :], lhsT=ones[:],
                                       rhs=Rt[p][:, i, :], start=False, stop=True)
                        for p in range(2):
                            for i in range(2):
                                pkk = "P%d%d" % (p, i)
                                OP("act", "activation", [pk(2 * p + i)], [pkk], out=Pt[p][:, i, :], in_=PSZ[p][:, i, :],
                                   func=AF.Exp, scale=-1.0)
                                if diag:
                                    OP("dve", "tensor_tensor", [pkk, "dmask"], [pkk], out=Pt[p][:, i, :],
                                       in0=Pt[p][:, i, :], in1=dmask[:, cp, :], op=ALU.mult)
                        for p in range(2):
                            if stp + 1 < nsteps:
                                zmm(oj, p, kb - 1)
                        for p in range(2):
                            for i in range(2):
                                hcol = slice((p * 2 + i) * 64, (p * 2 + i + 1) * 64)
                                OP("pe", "matmul", [("DV", kb // 4), "P%d%d" % (p, i)], [pk(4 + p)],
                                   PS[4 + p][i * 64:(i + 1) * 64, :], lhsT=DV[:, kb, hcol], rhs=Pt[p][:, i, :],
                                   start=(stp == 0), stop=(stp == nsteps - 1))
                        if stp + 1 < nsteps:
                            for p in range(2):
                                if stp == 0:
                                    OP("dve", "tensor_copy", ["sp%d0" % p, "sp%d1" % p], ["R%d" % p], out=Rt[p][:],
                                       in_=spt[p][:])
                                else:
                                    OP("dve", "tensor_tensor", ["sp%d0" % p, "sp%d1" % p, "R%d" % p], ["R%d" % p],
                                       out=Rt[p][:], in0=Rt[p][:], in1=spt[p][:], op=ALU.add)

                    it = 0
                    OWNS = OWN if 'S' not in SKIP else []
                    for oj, g in enumerate(OWNS):
                        while it < end_idx[g]:
                            items[it]()
                            it += 1
                        sweeping[0] = True
                        limit = end_idx[OWNS[oj + 1]] if oj + 1 < len(OWNS) else len(items)
                        n_items = limit - it
                        nsteps = 4 * g + 4
                        done = 0
                        for p in range(2):
                            zmm(oj, p, 4 * g + 3)
                        for stp in range(nsteps):
                            sweep_step(oj, g, stp, nsteps)
                            want = ((stp + 1) * n_items) // nsteps
                            while done < want:
                                items[it]()
                                it += 1
                                done += 1
                        for p in range(2):
                            pg = hp * 2 + p
                            OP("dve", "tensor_tensor", [pk(4 + p), ("VsT", oj)], [("yT", pg, oj)],
                               out=yT[:, pg, oj * 512:(oj + 1) * 512], in0=PS[4 + p][:], in1=VsT[:, p, oj, :], op=ALU.add)
                    while it < len(items):
                        items[it]()
                        it += 1
                S.barrier_all()


        for _ph in os.environ.get('KORDER', 'AH'):
            (pass_A if _ph == 'A' else pass_H)()

        with ExitStack() as cs:
            WC = sb("WC", [128, 8, 3072], BF16, cs)
            load_win(WC, "WC", 0, C_SBG, 512)
            load_win(WC, "WC", 512, C_HGG, 512)
            load_win(WC, "WC", 1024, C_GSB, 1024)
            load_win(WC, "WC", 2048, C_GHG, 1024)
            Wso = sb("Wso", [128, 4, D], BF16, cs)
            Who = sb("Who", [128, 4, D], BF16, cs)
            Wo = sb("Wo", [128, 8, D], BF16, cs)
            for q in range(4):
                for q0 in range(0, 1024, WCH):
                    load_w(Wso[:, q, q0:q0 + WCH], "Wso", w_sbo, q * 128, q0, WCH)
                    load_w(Who[:, q, q0:q0 + WCH], "Who", w_hgo, q * 128, q0, WCH)
            for c in range(8):
                for q0 in range(0, 1024, WCH):
                    load_w(Wo[:, c, q0:q0 + WCH], "Wo", w_out, c * 128, q0, WCH)
            fng = cload("fng", [128, D], F32, fng_d, cs)
            ysbT = sb("ysbT", [128, 4, 512], BF16, cs)
            yhgT = sb("yhgT", [128, 4, 512], BF16, cs)
            yhg2 = [sb("yhg%d" % i, [128, 512], BF16, cs) for i in range(2)]
            mT = sb("mT", [128, 8, 512], BF16, cs)
            c1b = [sb("c1_%d" % i, [128, 512], F32, cs) for i in range(2)]
            c2b = [sb("c2_%d" % i, [128, 512], F32, cs) for i in range(2)]
            hnb = [sb("hn%d" % i, [128, D], F32, cs) for i in range(2)]
            stc = sb("stc", [128, 4], F32, cs)

            def sigmoid_from(bank, tmp, tkey):
                OP("act", "activation", [pk(bank)], [tkey], out=tmp[:], in_=PS[bank][:], func=AF.Exp, scale=-1.0)
                recip1p(tmp[:], tkey)

            OWNC = OWN if 'C' not in SKIP else []
            cur = token_group_parts(OWNC[0], 0) if OWNC else None
            if cur is not None:
                for p_ in cur[2]:
                    p_()
            for oj, g in enumerate(OWNC):
                ut, ukey = cur[0], cur[1]
                nxt = token_group_parts(OWNC[oj + 1], oj + 1) if oj + 1 < len(OWNC) else None
                def part_sb():
                    for pg in range(4):
                        b = pg % 2
                        c1, k1 = c1b[pg % 2], "c1_%d" % (pg % 2)
                        proj_fm(b, WC, "WC", pg * 128, ut, ukey)
                        sigmoid_from(b, c1, k1)
                        OP("dve", "tensor_tensor", [pk(b), k1], [k1], out=c1[:], in0=PS[b][:], in1=c1[:], op=ALU.mult)
                        OP("dve", "tensor_tensor", [k1, ("yT", pg, oj)], ["ysbT"], out=ysbT[:, pg, :], in0=c1[:],
                           in1=yT[:, pg, oj * 512:(oj + 1) * 512], op=ALU.mult)
                def part_hg():
                    for tb in range(4):
                        b = 2 + tb % 2
                        c2, k2 = c2b[tb % 2], "c2_%d" % (tb % 2)
                        yhg, ky = yhg2[tb % 2], "yhg%d" % (tb % 2)
                        proj_tm(b, ut, ukey, 1, tb, WC, "WC", 512, 512)
                        sigmoid_from(b, c2, k2)
                        OP("dve", "tensor_tensor", [pk(b), k2], [k2], out=c2[:], in0=PS[b][:], in1=c2[:], op=ALU.mult)
                        OP("dve", "tensor_tensor", [k2, "onall"], [ky], out=yhg[:], in0=c2[:],
                           in1=onall[:, oj * 4 + tb, :], op=ALU.mult)
                        for q in range(4):
                            OP("pe", "transpose", [ky, "ident"], ["ptb"], out=PTB[:, q, :], in_=yhg[:, q * 128:(q + 1) * 128],
                               identity=ident[:])
                        OP("act", "copy", ["ptb"], ["yhgT"], out=yhgT[:, :, tb * 128:(tb + 1) * 128], in_=PTB[:, 0:4, :])
                def w_cc(cc):
                    cs_ = slice(cc * 128, (cc + 1) * 128)
                    bs = [(4 * cc + k) % 7 for k in range(4)]
                    c1, k1 = c1b[cc % 2], "c1_%d" % (cc % 2)
                    c2, k2 = c2b[cc % 2], "c2_%d" % (cc % 2)
                    proj_fm(bs[0], WC, "WC", 1024 + cc * 128, ut, ukey)
                    proj_fm(bs[1], WC, "WC", 2048 + cc * 128, ut, ukey)
                    for q in range(4):
                        OP("pe", "matmul", ["Wso", "ysbT"], [pk(bs[2])], PS[bs[2]][:], lhsT=Wso[:, q, cs_], rhs=ysbT[:, q, :],
                           start=(q == 0), stop=(q == 3))
                    for q in range(4):
                        OP("pe", "matmul", ["Who", "yhgT"], [pk(bs[3])], PS[bs[3]][:], lhsT=Who[:, q, cs_], rhs=yhgT[:, q, :],
                           start=(q == 0), stop=(q == 3))
                    sigmoid_from(bs[0], c1, k1)
                    sigmoid_from(bs[1], c2, k2)
                    OP("dve", "tensor_tensor", [pk(bs[2]), k1], [k1], out=c1[:], in0=PS[bs[2]][:], in1=c1[:], op=ALU.mult)
                    OP("dve", "tensor_tensor", [pk(bs[3]), k2], [k2], out=c2[:], in0=PS[bs[3]][:], in1=c2[:], op=ALU.mult)
                    OP("dve", "tensor_tensor", [k1, k2], [("mT", cc)], out=mT[:, cc, :], in0=c1[:], in1=c2[:], op=ALU.add)
                def part_delta():
                    for tb in range(4):
                        o_i = tb % 2
                        hn = hnb[o_i]
                        hk = "hn%d" % o_i
                        xi = load_x(g * 512 + tb * 128)
                        for half in range(2):
                            bk = half
                            for c in range(8):
                                OP("pe", "matmul", [("mT", c), "Wo"], [pk(bk)], PS[bk][:], lhsT=mT[:, c, tb * 128:(tb + 1) * 128],
                                   rhs=Wo[:, c, half * 512:(half + 1) * 512], start=(c == 0), stop=(c == 7))
                            OP("dve", "tensor_tensor", [pk(bk), "xt%d" % xi], [hk], out=hn[:, half * 512:(half + 1) * 512],
                               in0=PS[bk][:], in1=xt[xi][:, half * 512:(half + 1) * 512], op=ALU.add)
                        rstd_chain(hn[:], hk, stc, "stc")
                        OP("dve", "scalar_tensor_tensor", [hk, "stc", "fng"], [hk], out=hn[:], in0=hn[:],
                           scalar=stc[:, 3:4], in1=fng[:], op0=ALU.mult, op1=ALU.mult)
                        row0 = oj * 512 + tb * 128
                        S.dma("out%d" % o_i, [hk], [], out_d[row0:row0 + 128, :], hn[:])
                interleave(nxt, [[part_sb], [part_hg], [lambda: [w_cc(c_) for c_ in range(4)]],
                                 [lambda: [w_cc(c_) for c_ in range(4, 8)], part_delta]])
                cur = nxt
            S.barrier_all()
        S.emit()
        es_mem.pop_all()
    return nc


NG_FULL = 17
OWN_FULL = [4, 8, 12, 16]


def make_in_maps(x, meta, norm_g, w_in, w_sb_out, w_hg_out, w_out, hg_norm_g, hg_lb_logits, final_norm_g,
                 NG=NG_FULL, OWN=OWN_FULL, ncores_per_batch=4):
    B = x.shape[0]
    consts = host_consts()
    f32 = np.float32
    shared = {
        "w_in": np.ascontiguousarray(w_in[0], f32),
        "w_sb_out": np.ascontiguousarray(w_sb_out[0], f32),
        "w_hg_out": np.ascontiguousarray(w_hg_out[0], f32),
        "w_out": np.ascontiguousarray(w_out[0], f32),
        "ngT": np.ascontiguousarray(norm_g[0].reshape(8, 128).T, f32),
        "hgn_bc": np.ascontiguousarray(np.broadcast_to(hg_norm_g[0][None, :], (128, 512)), f32),
        "fng_bc": np.ascontiguousarray(np.broadcast_to(final_norm_g[None, :], (128, D)), f32),
        "lbl_bc": np.ascontiguousarray(np.broadcast_to(hg_lb_logits[None, :, :], (128, 2, 512)), f32),
        "lblT": np.ascontiguousarray(hg_lb_logits.reshape(2, 4, 128).transpose(2, 0, 1), f32),
    }
    shared.update(consts)
    in_maps = []
    nmeta = meta.shape[0]
    for b in range(B):
        G = np.concatenate([np.zeros((512 - nmeta, D), f32), meta.astype(f32), x[b].astype(f32)], axis=0)
        for r in range(ncores_per_batch):
            nz = (ncores_per_batch - 1 - r) * 512
            nreal = NG * 512 - nz
            xl = np.concatenate([np.zeros((nz, D), f32), G[:nreal]], axis=0)
            m = dict(shared)
            m["xloc"] = np.ascontiguousarray(xl)
            in_maps.append(m)
    return in_maps


_NC_CACHE = {}


def kernel(x, meta, norm_g, w_in, w_sb_out, w_hg_out, w_out, hg_norm_g, hg_lb_logits, final_norm_g):
    x = np.asarray(x)
    B, SEQ, _ = x.shape
    in_maps = make_in_maps(x, np.asarray(meta), np.asarray(norm_g), np.asarray(w_in), np.asarray(w_sb_out),
                           np.asarray(w_hg_out), np.asarray(w_out), np.asarray(hg_norm_g),
                           np.asarray(hg_lb_logits), np.asarray(final_norm_g))
    key = (NG_FULL, tuple(OWN_FULL))
    if key not in _NC_CACHE:
        _NC_CACHE[key] = build(NG_FULL, OWN_FULL)
    nc = _NC_CACHE[key]
    res = run_bass_kernel_spmd(nc, in_maps, core_ids=list(range(len(in_maps))))
    out = np.empty((B, SEQ, D), np.float32)
    for b in range(B):
        for r in range(4):
            o = np.asarray(res.results[b * 4 + r]["out"])
            for j in range(4):
                row = (4 * j + r) * 512
                out[b, row:row + 512] = o[j * 512:(j + 1) * 512]
    return out
```

```python
import os
import numpy as np
import ml_dtypes
from contextlib import ExitStack
import concourse.bass as bass
import concourse.mybir as mybir
from concourse.bass_utils import run_bass_kernel_spmd

F32 = mybir.dt.float32
BF16 = mybir.dt.bfloat16
AF = mybir.ActivationFunctionType
ALU = mybir.AluOpType
AX = mybir.AxisListType

D = 1024
EPS = 1e-6
SB_SCALE = 64 ** -0.5
HG_SCALE = 128 ** -0.5
C_SBQ, C_SBK, C_SBV, C_SBG, C_HGQ, C_HGF, C_HGI, C_HGG, C_GSB, C_GHG = (
    0, 512, 1024, 1536, 2048, 2560, 3072, 3584, 4096, 5120)


class Sync:
    ENG = ("pe", "act", "dve", "pool", "sp")

    def __init__(self, nc, es):
        self.nc = nc
        self.es = es
        self.sem = {e: es.enter_context(nc.semaphore("s_" + e)) for e in self.ENG}
        self.cnt = {e: 0 for e in self.ENG}
        self.prog = {e: [] for e in self.ENG}
        self.waited = {e: {} for e in self.ENG}
        self.bufs = {}
        self.dma_sems = {}
        self.alias = {}

    def _expand(self, keys):
        out = []
        for k in keys:
            out.extend(self.alias.get(k, (k,)))
        return out

    def _buf(self, k):
        b = self.bufs.get(k)
        if b is None:
            b = self.bufs[k] = ({}, {})
        return b

    def _deps(self, eng, reads, writes):
        deps = {}

        def add(d):
            for s, (sem, v) in d.items():
                if deps.get(s, (None, -1))[1] < v:
                    deps[s] = (sem, v)
        for k in reads:
            add(self._buf(k)[0])
        for k in writes:
            w, r = self._buf(k)
            add(w)
            add(r)
        for s, (sem, v) in deps.items():
            if eng == "pe" and s == "pe":
                continue
            if self.waited[eng].get(s, -1) >= v:
                continue
            self.waited[eng][s] = v
            self.prog[eng].append(("wait", sem, v))

    def _commit(self, tokname, tok, reads, writes):
        for k in reads:
            self._buf(k)[1][tokname] = tok
        for k in writes:
            w, r = self._buf(k)
            w[tokname] = tok
            r.clear()

    def op(self, eng, meth, r, w, *a, **k):
        r = self._expand(r)
        self._deps(eng, r, w)
        self.cnt[eng] += 1
        sem = self.sem[eng]
        self.prog[eng].append(("op", (meth, a, k), sem, 1))
        self._commit(eng, (sem, self.cnt[eng]), r, w)

    def dma(self, slot, r, w, out, in_, eng="sp"):
        if slot not in self.dma_sems:
            self.dma_sems[slot] = [self.es.enter_context(self.nc.semaphore("d_" + slot)), 0]
        self._deps(eng, r, w)
        ds = self.dma_sems[slot]
        ds[1] += 16
        self.prog[eng].append(("op", ("dma_start", (), dict(out=out, in_=in_)), ds[0], 16))
        self._commit("dma_" + slot, (ds[0], ds[1]), r, w)

    def barrier_all(self):
        toks = {e: (self.sem[e], self.cnt[e]) for e in self.ENG if self.cnt[e] > 0}
        for s, ds in self.dma_sems.items():
            toks["dma_" + s] = (ds[0], ds[1])
        for e in self.ENG:
            for s, (sem, v) in toks.items():
                if s == e or v == 0:
                    continue
                if self.waited[e].get(s, -1) >= v:
                    continue
                self.waited[e][s] = v
                self.prog[e].append(("wait", sem, v))
        self.bufs = {}

    def emit(self):
        nc = self.nc
        emap = {"pe": "tensor", "act": "scalar", "dve": "vector", "pool": "gpsimd", "sp": "sync"}
        with nc.Block() as block:
            for e in self.ENG:
                prog = self.prog[e]

                def body(engine, prog=prog):
                    for it in prog:
                        if it[0] == "wait":
                            engine.wait_ge(it[1], it[2])
                        else:
                            meth, a, k = it[1]
                            getattr(engine, meth)(*a, **k).then_inc(it[2], it[3])
                getattr(block, emap[e])(body)


def host_consts():
    s = np.arange(128)[:, None]
    t = np.arange(128)[None, :]
    same = (s // 64) == (t // 64)
    c = {}
    c["ident"] = np.eye(128).astype(ml_dtypes.bfloat16)
    c["triC"] = (same & (s <= t)).astype(np.float32)
    c["triSU"] = (same & (s > t)).astype(np.float32)
    c["triK"] = (s >= t).astype(ml_dtypes.bfloat16)
    c["ones"] = np.ones((128, 128), ml_dtypes.bfloat16)
    dm = np.zeros((128, 4, 4, 128), np.float32)
    for cp in range(4):
        for cq in range(4):
            if cp < cq:
                dm[:, cp, cq, :] = 1.0
            elif cp == cq:
                dm[:, cp, cq, :] = (s < t)
    c["dmask"] = dm.reshape(128, 4, 512).astype(ml_dtypes.bfloat16)
    return c


def build(NG, OWN):
    NOWN = len(OWN)
    T = NG * 512
    NB = NG * 4
    nc = bass.Bass("TRN2", target_bir_lowering=False)

    def din(name, shape, dt=F32):
        return nc.dram_tensor(name, list(shape), dt, kind="ExternalInput").ap()
    xloc = din("xloc", [T, D])
    w_in = din("w_in", [D, 6144])
    w_sbo = din("w_sb_out", [512, D])
    w_hgo = din("w_hg_out", [512, D])
    w_out = din("w_out", [D, D])
    ngT_d = din("ngT", [128, 8])
    hgn_d = din("hgn_bc", [128, 512])
    fng_d = din("fng_bc", [128, D])
    lbl_d = din("lbl_bc", [128, 2, 512])
    lblT_d = din("lblT", [128, 2, 4])
    ident_d = din("ident", [128, 128], BF16)
    triC_d = din("triC", [128, 128])
    triSU_d = din("triSU", [128, 128])
    triK_d = din("triK", [128, 128], BF16)
    ones_d = din("ones", [128, 128], BF16)
    dmask_d = din("dmask", [128, 4, 512], BF16)
    out_d = nc.dram_tensor("out", [NOWN * 512, D], F32, kind="ExternalOutput").ap()

    with ExitStack() as es:
        S = Sync(nc, es)
        OP = S.op

        es_mem = ExitStack()

        def sb(name, shape, dt, scope=es_mem):
            return scope.enter_context(nc.sbuf_tensor(name, list(shape), dt))

        PS01 = es_mem.enter_context(nc.psum_tensor("ps01", [128, 2, 512], F32))
        PS23 = es_mem.enter_context(nc.psum_tensor("ps23", [128, 2, 512], F32))
        PS = [PS01[:, 0, :], PS01[:, 1, :], PS23[:, 0, :], PS23[:, 1, :]]
        PS += [es_mem.enter_context(nc.psum_tensor("ps%d" % i, [128, 512], F32))[:] for i in range(4, 7)]
        PTB = es_mem.enter_context(nc.psum_tensor("ptb", [128, 8, 128], BF16))
        PSZ = [PS01, PS23]

        def pk(i):
            return ("ps", i)

        def cload(name, shape, dt, src, scope=None):
            scope = es_mem if scope is None else scope
            tl = sb(name + "_s", shape, dt, scope)
            S.dma("c_" + name, [], [name], tl[:], src)
            return tl

        ident = cload("ident", [128, 128], BF16, ident_d)
        triC = cload("triC", [128, 128], F32, triC_d)
        triSU = cload("triSU", [128, 128], F32, triSU_d)
        triK = cload("triK", [128, 128], BF16, triK_d)
        ones = cload("ones", [128, 128], BF16, ones_d)
        dmask = cload("dmask", [128, 4, 512], BF16, dmask_d)
        ngT = cload("ngT", [128, 8], F32, ngT_d)

        yT = sb("yT", [128, 4, NOWN * 512], BF16)
        onall = sb("onall", [128, NOWN * 4, 512], BF16)

        NXS = 3
        xt = [sb("xt%d" % i, [128, D], F32) for i in range(NXS)]
        junkd = sb("junkd", [128, D], BF16)
        xs = [sb("xs%d" % i, [128, D], BF16) for i in range(2)]
        uT = [sb("uT%d" % i, [128, 8, 513], BF16) for i in range(2)]
        st = [sb("st%d" % i, [128, 4], F32) for i in range(NXS)]
        NWS = 4
        WCH = 512
        wst = [sb("wst%d" % i, [128, WCH], F32) for i in range(NWS)]
        wst_i = [0]
        x_i = [0]

        def load_w(dst, dkey, src, row0, col0, ncols, gcol=None):
            i = wst_i[0] % NWS
            on_act = (wst_i[0] % 2) == 1
            wst_i[0] += 1
            if dkey not in S.alias:
                S.alias[dkey] = [(dkey, 0), (dkey, 1)]
            dkey = (dkey, 1 if on_act else 0)
            S.dma("wst%d" % i, [], ["wst%d" % i], wst[i][:, 0:ncols], src[row0:row0 + 128, col0:col0 + ncols])
            if gcol is None:
                if on_act:
                    OP("act", "copy", ["wst%d" % i], [dkey], out=dst, in_=wst[i][:, 0:ncols])
                else:
                    OP("dve", "tensor_copy", ["wst%d" % i], [dkey], out=dst, in_=wst[i][:, 0:ncols])
            else:
                if on_act:
                    OP("act", "activation", ["wst%d" % i, "ngT"], [dkey], out=dst, in_=wst[i][:, 0:ncols],
                       func=AF.Copy, scale=gcol)
                else:
                    OP("dve", "tensor_scalar", ["wst%d" % i, "ngT"], [dkey], out=dst, in0=wst[i][:, 0:ncols],
                       scalar1=gcol, scalar2=None, op0=ALU.mult)

        def load_win(Wt, wkey, off, col0, ncols):
            for c in range(8):
                for q0 in range(0, ncols, WCH):
                    n = min(WCH, ncols - q0)
                    load_w(Wt[:, c, off + q0:off + q0 + n], wkey, w_in, c * 128, col0 + q0, n, gcol=ngT[:, c:c + 1])

        mhalf = sb("mhalf", [128, 1], F32)
        OP("pool", "memset", [], ["mhalf"], mhalf[:], -0.5)

        def rstd_chain(src, srckey, stt, sttkey, n=D, noact=False):
            OP("dve", "scalar_tensor_tensor", [srckey], [sttkey], out=junkd[:, 0:n], in0=src, scalar=1.0, in1=src,
               op0=ALU.mult, op1=ALU.mult, accum_out=stt[:, 0:1])
            OP("dve", "tensor_scalar", [sttkey], [sttkey], out=stt[:, 1:2], in0=stt[:, 0:1], scalar1=1.0 / n,
               scalar2=EPS, op0=ALU.mult, op1=ALU.add)
            if noact:
                OP("pool", "tensor_tensor", [sttkey, "mhalf"], [sttkey], out=stt[:, 3:4], in0=stt[:, 1:2], in1=mhalf[:],
                   op=ALU.pow)
                return
            OP("act", "activation", [sttkey], [sttkey], out=stt[:, 2:3], in_=stt[:, 1:2], func=AF.Ln)
            OP("act", "activation", [sttkey], [sttkey], out=stt[:, 3:4], in_=stt[:, 2:3], func=AF.Exp, scale=-0.5)

        def load_x(row0):
            i = x_i[0] % NXS
            x_i[0] += 1
            S.dma("x%d" % i, [], ["xt%d" % i], xt[i][:], xloc[row0:row0 + 128, :])
            return i

        def token_group_parts(g, gi, evac="act"):
            ub = gi % 2
            ut = uT[ub]
            ukey = "uT%d" % ub
            slot = {}

            def st_load(tb):
                slot[tb] = load_x(g * 512 + tb * 128)

            def st_stats(tb):
                i = slot[tb]
                rstd_chain(xt[i][:], "xt%d" % i, st[i], "st%d" % i, noact=(evac != "act"))

            def st_rest(tb):
                i = slot[tb]
                j = tb % 2
                OP("dve", "tensor_scalar", ["xt%d" % i, "st%d" % i], ["xs%d" % j], out=xs[j][:], in0=xt[i][:],
                   scalar1=st[i][:, 3:4], scalar2=None, op0=ALU.mult)
                for c in range(8):
                    OP("pe", "transpose", ["xs%d" % j, "ident"], ["ptb"], out=PTB[:, c, :],
                       in_=xs[j][:, c * 128:(c + 1) * 128], identity=ident[:])
                if evac == "act":
                    OP("act", "copy", ["ptb"], [ukey], out=ut[:, :, 1 + tb * 128:1 + (tb + 1) * 128], in_=PTB[:])
                else:
                    OP("dve", "tensor_copy", ["ptb"], [ukey], out=ut[:, :, 1 + tb * 128:1 + (tb + 1) * 128], in_=PTB[:])
            def part(tb):
                if tb == 0:
                    st_load(0)
                    st_load(1)
                    st_stats(0)
                if tb + 2 < 4:
                    st_load(tb + 2)
                if tb + 1 < 4:
                    st_stats(tb + 1)
                st_rest(tb)
                if tb == 3:
                    if gi == 0:
                        OP("pool", "memset", [], [ukey], ut[:, :, 0:1], 0.0)
                    else:
                        pv = uT[(gi - 1) % 2]
                        OP("pool", "tensor_copy", ["uT%d" % ((gi - 1) % 2)], [ukey], out=ut[:, :, 0:1],
                           in_=pv[:, :, 512:513])
            return ut, ukey, [(lambda tb=tb: part(tb)) for tb in range(4)]

        def token_group(g, gi):
            ut, ukey, parts = token_group_parts(g, gi)
            for p_ in parts:
                p_()
            return ut, ukey

        def interleave(nxt, quarters):
            for q in range(4):
                if nxt is not None:
                    nxt[2][q]()
                for f in quarters[q]:
                    f()

        def proj_fm(bank, Wt, wkey, c0, ut, ukey, shift=0, ncols=128):
            for c in range(8):
                OP("pe", "matmul", [wkey, ukey], [pk(bank)], PS[bank][0:ncols, :], lhsT=Wt[:, c, c0:c0 + ncols],
                   rhs=ut[:, c, 1 - shift:513 - shift], start=(c == 0), stop=(c == 7))

        def proj_tm(bank, src, skey, off, tb, Wt, wkey, c0, ncols):
            for c in range(8):
                OP("pe", "matmul", [wkey, skey], [pk(bank)], PS[bank][:, 0:ncols],
                   lhsT=src[:, c, off + tb * 128:off + (tb + 1) * 128], rhs=Wt[:, c, c0:c0 + ncols],
                   start=(c == 0), stop=(c == 7))

        def recip1p(tl, key):
            OP("act", "activation", [key], [key], out=tl, in_=tl, func=AF.Ln, bias=1.0)
            OP("act", "activation", [key], [key], out=tl, in_=tl, func=AF.Exp, scale=-1.0)

        SKIP = os.environ.get('KSKIP', '')
        def pass_H():
            with ExitStack() as hs:
                WH = sb("WH", [128, 8, 1536], BF16, hs)
                load_win(WH, "WH", 0, C_HGQ, 1536)
                hgn = cload("hgn", [128, 512], F32, hgn_d, hs)
                lbl = cload("lbl", [128, 2, 512], F32, lbl_d, hs)
                lblT = cload("lblT", [128, 2, 4], F32, lblT_d, hs)
                oml = sb("oml", [128, 512], F32, hs)
                omlT = sb("omlT", [128, 4], F32, hs)
                for (dst, dk, src, sk) in ((oml, "oml", lbl, "lbl"), (omlT, "omlT", lblT, "lblT")):
                    OP("dve", "tensor_tensor", [sk], [dk], out=dst[:], in0=src[:, 0, :], in1=src[:, 1, :], op=ALU.subtract)
                    OP("act", "activation", [dk], [dk], out=dst[:], in_=dst[:], func=AF.Exp)
                    recip1p(dst[:], dk)

                tA = [sb("h_tA%d" % i, [128, 512], F32, hs) for i in range(4)]
                tB = [sb("h_tB%d" % i, [128, 512], F32, hs) for i in range(4)]
                tC = [sb("h_tC%d" % i, [128, 512], F32, hs) for i in range(4)]
                tD = [sb("h_tD%d" % i, [128, 512], F32, hs) for i in range(4)]
                kk = [sb("h_kk%d" % i, [128, 512], F32, hs) for i in range(4)]
                gtok = sb("h_g", [128, 4, 512], F32, hs)
                kdl2 = [sb("h_kdl%d" % i, [128, 4, 512], BF16, hs) for i in range(2)]
                vc2 = [sb("h_vc%d" % i, [128, 4, 512], BF16, hs) for i in range(2)]
                kdT2 = [sb("h_kdT%d" % i, [128, 4, 512], BF16, hs) for i in range(2)]
                dec2 = [sb("h_dec%d" % i, [128, 4, 8], F32, hs) for i in range(2)]
                qz = sb("h_qz", [128, 4, 4, 2, 128], BF16, hs)
                Sst = sb("h_S", [128, 4, 128], F32, hs)
                Sbf = sb("h_Sbf", [128, 2, 4, 2, 128], BF16, hs)
                atm4 = [sb("h_atm%d" % i, [128, 4, 128], BF16, hs) for i in range(2)]
                osb4 = [sb("h_o%d" % i, [128, 512], F32, hs) for i in range(2)]
                ost4 = [sb("h_ost%d" % i, [128, 16], F32, hs) for i in range(2)]
                triC4 = sb("h_triC4", [128, 4, 128], F32, hs)
                for h in range(4):
                    OP("pool", "tensor_copy", ["triC"], ["triC4"], out=triC4[:, h, :], in_=triC[:])
                OP("pool", "memset", [], [("qz", h) for h in range(4)], qz[:], 0.0)
                OP("pool", "memset", [], [("S", h) for h in range(4)], Sst[:], 0.0)

                cur = token_group_parts(0, 0)
                if 'H' not in SKIP:
                    for p_ in cur[2]:
                        p_()
                for g in range(NG if 'H' not in SKIP else 0):
                    own = g in OWN
                    oj = OWN.index(g) if own else -1
                    ut, ukey = cur[0], cur[1]
                    nxt = token_group_parts(g + 1, g + 1) if g + 1 < NG else None
                    gp = g % 2
                    kdl, vc, kdT, dec = kdl2[gp], vc2[gp], kdT2[gp], dec2[gp]
                    K_vc = lambda tb: ("vc", gp, tb)
                    K_kdl = lambda tb: ("kdl", gp, tb)
                    K_kdT = lambda h: ("kdT", gp, h)
                    K_dec = lambda h: ("dec", gp, h)
                    def part_T1():
                        for tb in range(4):
                            proj_tm(0, ut, ukey, 1, tb, WH, "WH", 512, 512)
                            proj_tm(1, ut, ukey, 1, tb, WH, "WH", 1024, 512)
                            ka = "tA%d" % tb
                            OP("act", "activation", [pk(0)], [ka], out=tA[tb][:], in_=PS[0][:], func=AF.Exp)
                            OP("act", "copy", [pk(1)], [K_vc(tb)], out=vc[:, tb, :], in_=PS[1][:])
                            recip1p(tA[tb][:], ka)
                            OP("dve", "tensor_tensor", [ka, "oml"], ["kk%d" % tb], out=kk[tb][:], in0=tA[tb][:], in1=oml[:],
                               op=ALU.mult)
                    def part_T23():
                        for tb in range(4):
                            OP("act", "activation", ["kk%d" % tb], [("gtok", tb)], out=gtok[:, tb, :], in_=kk[tb][:], func=AF.Ln,
                               scale=-1.0, bias=1.0)
                        for tb in range(4):
                            OP("pe", "matmul", ["triSU", ("gtok", tb)], [pk(2)], PS[2][:], lhsT=triSU[:], rhs=gtok[:, tb, :],
                               start=True, stop=True)
                            OP("act", "activation", [pk(2)], ["tB%d" % tb], out=tB[tb][:], in_=PS[2][:], func=AF.Exp)
                            OP("dve", "tensor_tensor", ["kk%d" % tb, "tB%d" % tb], [K_kdl(tb)], out=kdl[:, tb, :], in0=kk[tb][:],
                               in1=tB[tb][:], op=ALU.mult)
                            for h in range(4):
                                OP("pe", "matmul", ["triC", ("gtok", tb)], [pk(3 + h)], PS[3 + h][:, tb * 128:(tb + 1) * 128],
                                   lhsT=gtok[:, tb, h * 128:(h + 1) * 128], rhs=triC[:], start=True, stop=True)
                    def part_F():
                        for h in range(4):
                            proj_fm(0, WH, "WH", 512 + h * 128, ut, ukey)
                            ka, kb_ = "tA%d" % h, "tB%d" % h
                            OP("act", "activation", [pk(0)], [ka], out=tA[h][:], in_=PS[0][:], func=AF.Exp)
                            recip1p(tA[h][:], ka)
                            OP("act", "activation", [pk(3 + h)], [kb_], out=tB[h][:], in_=PS[3 + h][:], func=AF.Exp, scale=-1.0)
                            OP("dve", "scalar_tensor_tensor", [ka, kb_, "omlT"], [K_kdT(h)], out=kdT[:, h, :], in0=tA[h][:],
                               scalar=omlT[:, h:h + 1], in1=tB[h][:], op0=ALU.mult, op1=ALU.mult)
                            OP("act", "activation", [pk(3 + h)], [K_dec(h)], out=dec[:, h, :],
                               in_=PS[3 + h][:].rearrange("p (c t) -> p c t", t=64)[:, :, 63], func=AF.Exp)
                        if own:
                            for h in range(4):
                                qb = h % 2
                                kc, kd = "tC%d" % h, "tD%d" % h
                                proj_fm(qb, WH, "WH", h * 128, ut, ukey)
                                OP("act", "activation", [pk(qb)], [kc], out=tC[h][:], in_=PS[qb][:], func=AF.Exp, scale=-1.0)
                                recip1p(tC[h][:], kc)
                                OP("dve", "tensor_tensor", [pk(qb), kc], [kc], out=tC[h][:], in0=PS[qb][:], in1=tC[h][:],
                                   op=ALU.mult)
                                OP("act", "activation", [pk(3 + h)], [kd], out=tD[h][:], in_=PS[3 + h][:], func=AF.Exp)
                                for ci in range(2):
                                    v4c = tC[h][:].rearrange("p (b c t) -> p b c t", b=4, c=2)[:, :, ci, :]
                                    v4d = tD[h][:].rearrange("p (b c t) -> p b c t", b=4, c=2)[:, :, ci, :]
                                    OP("dve", "scalar_tensor_tensor", [kc, kd], [("qz", h)],
                                       out=qz[:, h, :, ci, ci * 64:(ci + 1) * 64], in0=v4c, scalar=HG_SCALE, in1=v4d,
                                       op0=ALU.mult, op1=ALU.mult)
                    def part_R():
                        for cch in range(8):
                            tb, ci = cch // 2, cch % 2
                            for h in range(4):
                                hc = slice(h * 128, (h + 1) * 128)
                                rows = slice(ci * 64, (ci + 1) * 64)
                                if own:
                                    OP("pool", "tensor_copy", [("S", h)], [("Sbf", h, tb % 2)], out=Sbf[:, tb % 2, h, ci, :],
                                       in_=Sst[:, h, :])
                                OP("pe", "matmul", [K_kdl(tb), K_vc(tb)], [pk(3 + h)], PS[3 + h][:, 0:128], lhsT=kdl[rows, tb, hc],
                                   rhs=vc[rows, tb, hc], start=True, stop=True)
                                OP("dve", "scalar_tensor_tensor", [("S", h), K_dec(h), pk(3 + h)], [("S", h)], out=Sst[:, h, :],
                                   in0=Sst[:, h, :], scalar=dec[:, h, cch:cch + 1], in1=PS[3 + h][:, 0:128],
                                   op0=ALU.mult, op1=ALU.add)
                            if own and ci == 1:
                                blk = slice(tb * 128, (tb + 1) * 128)
                                pb = tb % 2
                                atm_, osb_, ost_ = atm4[pb], osb4[pb], ost4[pb]
                                ak, okk, sk = "atm%d" % pb, "osb%d" % pb, "ost%d" % pb
                                for h in range(4):
                                    reg = PS[2][:, h * 128:(h + 1) * 128]
                                    OP("pe", "matmul", [K_kdT(h), ("qz", h)], [pk(2)], reg, lhsT=kdT[:, h, blk],
                                       rhs=qz[:, h, tb, 0, :], start=True, stop=False)
                                    OP("pe", "matmul", [K_kdT(h), ("qz", h)], [pk(2)], reg, lhsT=kdT[:, h, blk],
                                       rhs=qz[:, h, tb, 1, :], start=False, stop=True)
                                OP("dve", "tensor_tensor", [pk(2), "triC4"], [ak], out=atm_[:],
                                   in0=PS[2][:].rearrange("p (h t) -> p h t", h=4), in1=triC4[:], op=ALU.mult)
                                for h in range(4):
                                    hc = slice(h * 128, (h + 1) * 128)
                                    reg = PS[1][:, hc]
                                    OP("pe", "matmul", [ak, K_vc(tb)], [pk(1)], reg, lhsT=atm_[:, h, :], rhs=vc[:, tb, hc],
                                       start=True, stop=False)
                                    OP("pe", "matmul", [("qz", h), ("Sbf", h, pb)], [pk(1)], reg, lhsT=qz[:, h, tb, 0, :],
                                       rhs=Sbf[:, pb, h, 0, :], start=False, stop=False)
                                    OP("pe", "matmul", [("qz", h), ("Sbf", h, pb)], [pk(1)], reg, lhsT=qz[:, h, tb, 1, :],
                                       rhs=Sbf[:, pb, h, 1, :], start=False, stop=True)
                                OP("act", "copy", [pk(1)], [okk], out=osb_[:], in_=PS[1][:])
                                for h in range(4):
                                    hc = slice(h * 128, (h + 1) * 128)
                                    OP("dve", "scalar_tensor_tensor", [okk], [sk], out=junkd[:, 0:128], in0=osb_[:, hc],
                                       scalar=1.0, in1=osb_[:, hc], op0=ALU.mult, op1=ALU.mult, accum_out=ost_[:, h:h + 1])
                                OP("dve", "tensor_scalar", [sk], [sk], out=ost_[:, 4:8], in0=ost_[:, 0:4], scalar1=1.0 / 128,
                                   scalar2=EPS, op0=ALU.mult, op1=ALU.add)
                                OP("act", "activation", [sk], [sk], out=ost_[:, 8:12], in_=ost_[:, 4:8], func=AF.Ln)
                                OP("act", "activation", [sk], [sk], out=ost_[:, 12:16], in_=ost_[:, 8:12], func=AF.Exp,
                                   scale=-0.5)
                                for h in range(4):
                                    hc = slice(h * 128, (h + 1) * 128)
                                    OP("dve", "scalar_tensor_tensor", [okk, sk, "hgn"], ["onall"],
                                       out=onall[:, oj * 4 + tb, hc], in0=osb_[:, hc], scalar=ost_[:, 12 + h:13 + h],
                                       in1=hgn[:, hc], op0=ALU.mult, op1=ALU.mult)
                    interleave(nxt, [[part_T1], [part_T23], [part_F], [part_R]])
                    cur = nxt
                S.barrier_all()


        def pass_A():
            with ExitStack() as as_:
                WA = sb("WA", [128, 8, 768], BF16, as_)
                KT = sb("KT", [128, 2, T], BF16, as_)
                DV = sb("DV", [128, NB, 256], BF16, as_)
                QT = sb("QT", [128, 2, NOWN, 512], BF16, as_)
                VsT = sb("VsT", [128, 2, NOWN, 512], BF16, as_)
                duT = sb("duT", [128, 8, 512], BF16, as_)
                Et = [sb("E%d" % p, [128, 2, 512], F32, as_) for p in range(2)]
                spt = [sb("sp%d" % p, [128, 2, 512], BF16, as_) for p in range(2)]
                Pt = [sb("P%d" % p, [128, 2, 512], BF16, as_) for p in range(2)]
                Rt = [sb("R%d" % p, [128, 2, 512], BF16, as_) for p in range(2)]

                for hp in range(2 if 'A' not in SKIP else 0):
                    h0 = hp * 4
                    load_win(WA, "WA", 0, C_SBQ + h0 * 64, 256)
                    load_win(WA, "WA", 256, C_SBK + h0 * 64, 256)
                    load_win(WA, "WA", 512, C_SBV + h0 * 64, 256)
                    bank_i = [0]
                    sweeping = [False]

                    def nb():
                        if sweeping[0]:
                            return 6
                        bank_i[0] = (bank_i[0] + 1) % 5
                        return (0, 1, 2, 3, 6)[bank_i[0]]

                    tg = [token_group_parts(g, g, evac="dve") for g in range(NG)]
                    items = list(tg[0][2])
                    end_idx = {}
                    for g in range(NG):
                        own = g in OWN
                        oj = OWN.index(g) if own else -1
                        ut, ukey = tg[g][0], tg[g][1]

                        def w_du(ut=ut, ukey=ukey):
                            OP("dve", "tensor_tensor", [ukey], ["duT"], out=duT[:], in0=ut[:, :, 0:512], in1=ut[:, :, 1:513],
                               op=ALU.subtract)

                        def w_kt(p, g=g, ut=ut, ukey=ukey):
                            b = nb()
                            proj_fm(b, WA, "WA", 256 + p * 128, ut, ukey)
                            OP("dve", "tensor_copy", [pk(b)], [("KT", g)], out=KT[:, p, g * 512:(g + 1) * 512], in_=PS[b][:])

                        def w_dv(tb, g=g):
                            b = nb()
                            proj_tm(b, duT, "duT", 0, tb, WA, "WA", 512, 256)
                            OP("dve", "tensor_copy", [pk(b)], [("DV", g)], out=DV[:, g * 4 + tb, :], in_=PS[b][:, 0:256])

                        def w_own(p, oj=oj, ut=ut, ukey=ukey):
                            b = nb()
                            proj_fm(b, WA, "WA", p * 128, ut, ukey)
                            OP("dve", "tensor_copy", [pk(b)], [("QT", oj)], out=QT[:, p, oj, :], in_=PS[b][:])
                            b = nb()
                            proj_fm(b, WA, "WA", 512 + p * 128, ut, ukey, shift=1)
                            OP("dve", "tensor_copy", [pk(b)], [("VsT", oj)], out=VsT[:, p, oj, :], in_=PS[b][:])
                        quarters = [[w_du, (lambda f=w_kt: f(0)), (lambda f=w_kt: f(1))],
                                    [(lambda f=w_dv: f(0)), (lambda f=w_dv: f(1))],
                                    [(lambda f=w_dv: f(2)), (lambda f=w_dv: f(3))],
                                    ([(lambda f=w_own: f(0)), (lambda f=w_own: f(1))] if own else [])]
                        for q in range(4):
                            if g + 1 < NG:
                                items.append(tg[g + 1][2][q])
                            items.extend(quarters[q])
                        end_idx[g] = len(items)

                    def zmm(oj, p, kb):
                        for i in range(2):
                            rows = slice(i * 64, (i + 1) * 64)
                            OP("pe", "matmul", [("KT", kb // 4), ("QT", oj)], [pk(2 * p + i)], PSZ[p][:, i, :],
                               lhsT=KT[rows, p, kb * 128:(kb + 1) * 128], rhs=QT[rows, p, oj, :], start=True, stop=True)

                    def sweep_step(oj, g, stp, nsteps):
                        kb = 4 * g + 3 - stp
                        diag = stp < 4
                        cp = 3 - stp
                        for p in range(2):
                            for i in range(2):
                                OP("act", "activation", [pk(2 * p + i)], ["E%d%d" % (p, i)], out=Et[p][:, i, :],
                                   in_=PSZ[p][:, i, :], func=AF.Exp, scale=SB_SCALE)
                        for p in range(2):
                            for i in range(2):
                                spk = "sp%d%d" % (p, i)
                                OP("act", "activation", ["E%d%d" % (p, i)], [spk], out=spt[p][:, i, :], in_=Et[p][:, i, :],
                                   func=AF.Ln, bias=1.0)
                                if diag:
                                    OP("dve", "tensor_tensor", [spk, "dmask"], [spk], out=spt[p][:, i, :],
                                       in0=spt[p][:, i, :], in1=dmask[:, cp, :], op=ALU.mult)
                        for p in range(2):
                            for i in range(2):
                                OP("pe", "matmul", ["triK", "sp%d%d" % (p, i)], [pk(2 * p + i)], PSZ[p][:, i, :], lhsT=triK[:],
                                   rhs=spt[p][:, i, :], start=True, stop=(stp == 0))
                                if stp > 0:
                                    OP("pe", "matmul", ["ones", "R%d" % p], [pk(2 * p + i)], PSZ[p][:, i, :], lhsT=ones[:],
                                       rhs=Rt[p][:, i, :], start=False, stop=True)
                        for p in range(2):
                            for i in range(2):
                                pkk = "P%d%d" % (p, i)
                                OP("act", "activation", [pk(2 * p + i)], [pkk], out=Pt[p][:, i, :], in_=PSZ[p][:, i, :],
                                   func=AF.Exp, scale=-1.0)
                                if diag:
                                    OP("dve", "tensor_tensor", [pkk, "dmask"], [pkk], out=Pt[p][:, i, :],
                                       in0=Pt[p][:, i, :], in1=dmask[:, cp, :], op=ALU.mult)
                        for p in range(2):
                            if stp + 1 < nsteps:
                                zmm(oj, p, kb - 1)
                        for p in range(2):
                            for i in range(2):
                                hcol = slice((p * 2 + i) * 64, (p * 2 + i + 1) * 64)
                                OP("pe", "matmul", [("DV", kb // 4), "P%d%d" % (p, i)], [pk(4 + p)],
                                   PS[4 + p][i * 64:(i + 1) * 64, :], lhsT=DV[:, kb, hcol], rhs=Pt[p][:, i, :],
                                   start=(stp == 0), stop=(stp == nsteps - 1))
                        if stp + 1 < nsteps:
                            for p in range(2):
                                if stp == 0:
                                    OP("dve", "tensor_copy", ["sp%d0" % p, "sp%d1" % p], ["R%d" % p], out=Rt[p][:],
                                       in_=spt[p][:])
                                else:
                                    OP("dve", "tensor_tensor", ["sp%d0" % p, "sp%d1" % p, "R%d" % p], ["R%d" % p],
                                       out=Rt[p][:], in0=Rt[p][:], in1=spt[p][:], op=ALU.add)

                    it = 0
                    OWNS = OWN if 'S' not in SKIP else []
                    for oj, g in enumerate(OWNS):
                        while it < end_idx[g]:
                            items[it]()
                            it += 1
                        sweeping[0] = True
                        limit = end_idx[OWNS[oj + 1]] if oj + 1 < len(OWNS) else len(items)
                        n_items = limit - it
                        nsteps = 4 * g + 4
                        done = 0
                        for p in range(2):
                            zmm(oj, p, 4 * g + 3)
                        for stp in range(nsteps):
                            sweep_step(oj, g, stp, nsteps)
                            want = ((stp + 1) * n_items) // nsteps
                            while done < want:
                                items[it]()
                                it += 1
                                done += 1
                        for p in range(2):
                            pg = hp * 2 + p
                            OP("dve", "tensor_tensor", [pk(4 + p), ("VsT", oj)], [("yT", pg, oj)],
                               out=yT[:, pg, oj * 512:(oj + 1) * 512], in0=PS[4 + p][:], in1=VsT[:, p, oj, :], op=ALU.add)
                    while it < len(items):
                        items[it]()
                        it += 1
                S.barrier_all()


        for _ph in os.environ.get('KORDER', 'AH'):
            (pass_A if _ph == 'A' else pass_H)()

        with ExitStack() as cs:
            WC = sb("WC", [128, 8, 3072], BF16, cs)
            load_win(WC, "WC", 0, C_SBG, 512)
            load_win(WC, "WC", 512, C_HGG, 512)
            load_win(WC, "WC", 1024, C_GSB, 1024)
            load_win(WC, "WC", 2048, C_GHG, 1024)
            Wso = sb("Wso", [128, 4, D], BF16, cs)
            Who = sb("Who", [128, 4, D], BF16, cs)
            Wo = sb("Wo", [128, 8, D], BF16, cs)
            for q in range(4):
                for q0 in range(0, 1024, WCH):
                    load_w(Wso[:, q, q0:q0 + WCH], "Wso", w_sbo, q * 128, q0, WCH)
                    load_w(Who[:, q, q0:q0 + WCH], "Who", w_hgo, q * 128, q0, WCH)
            for c in range(8):
                for q0 in range(0, 1024, WCH):
                    load_w(Wo[:, c, q0:q0 + WCH], "Wo", w_out, c * 128, q0, WCH)
            fng = cload("fng", [128, D], F32, fng_d, cs)
            ysbT = sb("ysbT", [128, 4, 512], BF16, cs)
            yhgT = sb("yhgT", [128, 4, 512], BF16, cs)
            yhg2 = [sb("yhg%d" % i, [128, 512], BF16, cs) for i in range(2)]
            mT = sb("mT", [128, 8, 512], BF16, cs)
            c1b = [sb("c1_%d" % i, [128, 512], F32, cs) for i in range(2)]
            c2b = [sb("c2_%d" % i, [128, 512], F32, cs) for i in range(2)]
            hnb = [sb("hn%d" % i, [128, D], F32, cs) for i in range(2)]
            stc = sb("stc", [128, 4], F32, cs)

            def sigmoid_from(bank, tmp, tkey):
                OP("act", "activation", [pk(bank)], [tkey], out=tmp[:], in_=PS[bank][:], func=AF.Exp, scale=-1.0)
                recip1p(tmp[:], tkey)

            OWNC = OWN if 'C' not in SKIP else []
            cur = token_group_parts(OWNC[0], 0) if OWNC else None
            if cur is not None:
                for p_ in cur[2]:
                    p_()
            for oj, g in enumerate(OWNC):
                ut, ukey = cur[0], cur[1]
                nxt = token_group_parts(OWNC[oj + 1], oj + 1) if oj + 1 < len(OWNC) else None
                def part_sb():
                    for pg in range(4):
                        b = pg % 2
                        c1, k1 = c1b[pg % 2], "c1_%d" % (pg % 2)
                        proj_fm(b, WC, "WC", pg * 128, ut, ukey)
                        sigmoid_from(b, c1, k1)
                        OP("dve", "tensor_tensor", [pk(b), k1], [k1], out=c1[:], in0=PS[b][:], in1=c1[:], op=ALU.mult)
                        OP("dve", "tensor_tensor", [k1, ("yT", pg, oj)], ["ysbT"], out=ysbT[:, pg, :], in0=c1[:],
                           in1=yT[:, pg, oj * 512:(oj + 1) * 512], op=ALU.mult)
                def part_hg():
                    for tb in range(4):
                        b = 2 + tb % 2
                        c2, k2 = c2b[tb % 2], "c2_%d" % (tb % 2)
                        yhg, ky = yhg2[tb % 2], "yhg%d" % (tb % 2)
                        proj_tm(b, ut, ukey, 1, tb, WC, "WC", 512, 512)
                        sigmoid_from(b, c2, k2)
                        OP("dve", "tensor_tensor", [pk(b), k2], [k2], out=c2[:], in0=PS[b][:], in1=c2[:], op=ALU.mult)
                        OP("dve", "tensor_tensor", [k2, "onall"], [ky], out=yhg[:], in0=c2[:],
                           in1=onall[:, oj * 4 + tb, :], op=ALU.mult)
                        for q in range(4):
                            OP("pe", "transpose", [ky, "ident"], ["ptb"], out=PTB[:, q, :], in_=yhg[:, q * 128:(q + 1) * 128],
                               identity=ident[:])
                        OP("act", "copy", ["ptb"], ["yhgT"], out=yhgT[:, :, tb * 128:(tb + 1) * 128], in_=PTB[:, 0:4, :])
                def w_cc(cc):
                    cs_ = slice(cc * 128, (cc + 1) * 128)
                    bs = [(4 * cc + k) % 7 for k in range(4)]
                    c1, k1 = c1b[cc % 2], "c1_%d" % (cc % 2)
                    c2, k2 = c2b[cc % 2], "c2_%d" % (cc % 2)
                    proj_fm(bs[0], WC, "WC", 1024 + cc * 128, ut, ukey)
                    proj_fm(bs[1], WC, "WC", 2048 + cc * 128, ut, ukey)
                    for q in range(4):
                        OP("pe", "matmul", ["Wso", "ysbT"], [pk(bs[2])], PS[bs[2]][:], lhsT=Wso[:, q, cs_], rhs=ysbT[:, q, :],
                           start=(q == 0), stop=(q == 3))
                    for q in range(4):
                        OP("pe", "matmul", ["Who", "yhgT"], [pk(bs[3])], PS[bs[3]][:], lhsT=Who[:, q, cs_], rhs=yhgT[:, q, :],
                           start=(q == 0), stop=(q == 3))
                    sigmoid_from(bs[0], c1, k1)
                    sigmoid_from(bs[1], c2, k2)
                    OP("dve", "tensor_tensor", [pk(bs[2]), k1], [k1], out=c1[:], in0=PS[bs[2]][:], in1=c1[:], op=ALU.mult)
                    OP("dve", "tensor_tensor", [pk(bs[3]), k2], [k2], out=c2[:], in0=PS[bs[3]][:], in1=c2[:], op=ALU.mult)
                    OP("dve", "tensor_tensor", [k1, k2], [("mT", cc)], out=mT[:, cc, :], in0=c1[:], in1=c2[:], op=ALU.add)
                def part_delta():
                    for tb in range(4):
                        o_i = tb % 2
                        hn = hnb[o_i]
                        hk = "hn%d" % o_i
                        xi = load_x(g * 512 + tb * 128)
                        for half in range(2):
                            bk = half
                            for c in range(8):
                                OP("pe", "matmul", [("mT", c), "Wo"], [pk(bk)], PS[bk][:], lhsT=mT[:, c, tb * 128:(tb + 1) * 128],
                                   rhs=Wo[:, c, half * 512:(half + 1) * 512], start=(c == 0), stop=(c == 7))
                            OP("dve", "tensor_tensor", [pk(bk), "xt%d" % xi], [hk], out=hn[:, half * 512:(half + 1) * 512],
                               in0=PS[bk][:], in1=xt[xi][:, half * 512:(half + 1) * 512], op=ALU.add)
                        rstd_chain(hn[:], hk, stc, "stc")
                        OP("dve", "scalar_tensor_tensor", [hk, "stc", "fng"], [hk], out=hn[:], in0=hn[:],
                           scalar=stc[:, 3:4], in1=fng[:], op0=ALU.mult, op1=ALU.mult)
                        row0 = oj * 512 + tb * 128
                        S.dma("out%d" % o_i, [hk], [], out_d[row0:row0 + 128, :], hn[:])
                interleave(nxt, [[part_sb], [part_hg], [lambda: [w_cc(c_) for c_ in range(4)]],
                                 [lambda: [w_cc(c_) for c_ in range(4, 8)], part_delta]])
                cur = nxt
            S.barrier_all()
        S.emit()
        es_mem.pop_all()
    return nc


NG_FULL = 17
OWN_FULL = [4, 8, 12, 16]


def make_in_maps(x, meta, norm_g, w_in, w_sb_out, w_hg_out, w_out, hg_norm_g, hg_lb_logits, final_norm_g,
                 NG=NG_FULL, OWN=OWN_FULL, ncores_per_batch=4):
    B = x.shape[0]
    consts = host_consts()
    f32 = np.float32
    shared = {
        "w_in": np.ascontiguousarray(w_in[0], f32),
        "w_sb_out": np.ascontiguousarray(w_sb_out[0], f32),
        "w_hg_out": np.ascontiguousarray(w_hg_out[0], f32),
        "w_out": np.ascontiguousarray(w_out[0], f32),
        "ngT": np.ascontiguousarray(norm_g[0].reshape(8, 128).T, f32),
        "hgn_bc": np.ascontiguousarray(np.broadcast_to(hg_norm_g[0][None, :], (128, 512)), f32),
        "fng_bc": np.ascontiguousarray(np.broadcast_to(final_norm_g[None, :], (128, D)), f32),
        "lbl_bc": np.ascontiguousarray(np.broadcast_to(hg_lb_logits[None, :, :], (128, 2, 512)), f32),
        "lblT": np.ascontiguousarray(hg_lb_logits.reshape(2, 4, 128).transpose(2, 0, 1), f32),
    }
    shared.update(consts)
    in_maps = []
    nmeta = meta.shape[0]
    for b in range(B):
        G = np.concatenate([np.zeros((512 - nmeta, D), f32), meta.astype(f32), x[b].astype(f32)], axis=0)
        for r in range(ncores_per_batch):
            nz = (ncores_per_batch - 1 - r) * 512
            nreal = NG * 512 - nz
            xl = np.concatenate([np.zeros((nz, D), f32), G[:nreal]], axis=0)
            m = dict(shared)
            m["xloc"] = np.ascontiguousarray(xl)
            in_maps.append(m)
    return in_maps


_NC_CACHE = {}


def kernel(x, meta, norm_g, w_in, w_sb_out, w_hg_out, w_out, hg_norm_g, hg_lb_logits, final_norm_g):
    x = np.asarray(x)
    B, SEQ, _ = x.shape
    in_maps = make_in_maps(x, np.asarray(meta), np.asarray(norm_g), np.asarray(w_in), np.asarray(w_sb_out),
                           np.asarray(w_hg_out), np.asarray(w_out), np.asarray(hg_norm_g),
                           np.asarray(hg_lb_logits), np.asarray(final_norm_g))
    key = (NG_FULL, tuple(OWN_FULL))
    if key not in _NC_CACHE:
        _NC_CACHE[key] = build(NG_FULL, OWN_FULL)
    nc = _NC_CACHE[key]
    res = run_bass_kernel_spmd(nc, in_maps, core_ids=list(range(len(in_maps))))
    out = np.empty((B, SEQ, D), np.float32)
    for b in range(B):
        for r in range(4):
            o = np.asarray(res.results[b * 4 + r]["out"])
            for j in range(4):
                row = (4 * j + r) * 512
                out[b, row:row + 512] = o[j * 512:(j + 1) * 512]
    return out
```

```python
import os
import numpy as np
import ml_dtypes
from contextlib import ExitStack
import concourse.bass as bass
import concourse.mybir as mybir
from concourse.bass_utils import run_bass_kernel_spmd

F32 = mybir.dt.float32
BF16 = mybir.dt.bfloat16
AF = mybir.ActivationFunctionType
ALU = mybir.AluOpType
AX = mybir.AxisListType

D = 1024
EPS = 1e-6
SB_SCALE = 64 ** -0.5
HG_SCALE = 128 ** -0.5
C_SBQ, C_SBK, C_SBV, C_SBG, C_HGQ, C_HGF, C_HGI, C_HGG, C_GSB, C_GHG = (
    0, 512, 1024, 1536, 2048, 2560, 3072, 3584, 4096, 5120)


class Sync:
    ENG = ("pe", "act", "dve", "pool", "sp")

    def __init__(self, nc, es):
        self.nc = nc
        self.es = es
        self.sem = {e: es.enter_context(nc.semaphore("s_" + e)) for e in self.ENG}
        self.cnt = {e: 0 for e in self.ENG}
        self.prog = {e: [] for e in self.ENG}
        self.waited = {e: {} for e in self.ENG}
        self.bufs = {}
        self.dma_sems = {}
        self.alias = {}

    def _expand(self, keys):
        out = []
        for k in keys:
            out.extend(self.alias.get(k, (k,)))
        return out

    def _buf(self, k):
        b = self.bufs.get(k)
        if b is None:
            b = self.bufs[k] = ({}, {})
        return b

    def _deps(self, eng, reads, writes):
        deps = {}

        def add(d):
            for s, (sem, v) in d.items():
                if deps.get(s, (None, -1))[1] < v:
                    deps[s] = (sem, v)
        for k in reads:
            add(self._buf(k)[0])
        for k in writes:
            w, r = self._buf(k)
            add(w)
            add(r)
        for s, (sem, v) in deps.items():
            if eng == "pe" and s == "pe":
                continue
            if self.waited[eng].get(s, -1) >= v:
                continue
            self.waited[eng][s] = v
            self.prog[eng].append(("wait", sem, v))

    def _commit(self, tokname, tok, reads, writes):
        for k in reads:
            self._buf(k)[1][tokname] = tok
        for k in writes:
            w, r = self._buf(k)
            w[tokname] = tok
            r.clear()

    def op(self, eng, meth, r, w, *a, **k):
        r = self._expand(r)
        self._deps(eng, r, w)
        self.cnt[eng] += 1
        sem = self.sem[eng]
        self.prog[eng].append(("op", (meth, a, k), sem, 1))
        self._commit(eng, (sem, self.cnt[eng]), r, w)

    def dma(self, slot, r, w, out, in_, eng="sp"):
        if slot not in self.dma_sems:
            self.dma_sems[slot] = [self.es.enter_context(self.nc.semaphore("d_" + slot)), 0]
        self._deps(eng, r, w)
        ds = self.dma_sems[slot]
        ds[1] += 16
        self.prog[eng].append(("op", ("dma_start", (), dict(out=out, in_=in_)), ds[0], 16))
        self._commit("dma_" + slot, (ds[0], ds[1]), r, w)

    def barrier_all(self):
        toks = {e: (self.sem[e], self.cnt[e]) for e in self.ENG if self.cnt[e] > 0}
        for s, ds in self.dma_sems.items():
            toks["dma_" + s] = (ds[0], ds[1])
        for e in self.ENG:
            for s, (sem, v) in toks.items():
                if s == e or v == 0:
                    continue
                if self.waited[e].get(s, -1) >= v:
                    continue
                self.waited[e][s] = v
                self.prog[e].append(("wait", sem, v))
        self.bufs = {}

    def emit(self):
        nc = self.nc
        emap = {"pe": "tensor", "act": "scalar", "dve": "vector", "pool": "gpsimd", "sp": "sync"}
        with nc.Block() as block:
            for e in self.ENG:
                prog = self.prog[e]

                def body(engine, prog=prog):
                    for it in prog:
                        if it[0] == "wait":
                            engine.wait_ge(it[1], it[2])
                        else:
                            meth, a, k = it[1]
                            getattr(engine, meth)(*a, **k).then_inc(it[2], it[3])
                getattr(block, emap[e])(body)


def host_consts():
    s = np.arange(128)[:, None]
    t = np.arange(128)[None, :]
    same = (s // 64) == (t // 64)
    c = {}
    c["ident"] = np.eye(128).astype(ml_dtypes.bfloat16)
    c["triC"] = (same & (s <= t)).astype(np.float32)
    c["triSU"] = (same & (s > t)).astype(np.float32)
    c["triK"] = (s >= t).astype(ml_dtypes.bfloat16)
    c["ones"] = np.ones((128, 128), ml_dtypes.bfloat16)
    dm = np.zeros((128, 4, 4, 128), np.float32)
    for cp in range(4):
        for cq in range(4):
            if cp < cq:
                dm[:, cp, cq, :] = 1.0
            elif cp == cq:
                dm[:, cp, cq, :] = (s < t)
    c["dmask"] = dm.reshape(128, 4, 512).astype(ml_dtypes.bfloat16)
    return c


def build(NG, OWN):
    NOWN = len(OWN)
    T = NG * 512
    NB = NG * 4
    nc = bass.Bass("TRN2", target_bir_lowering=False)

    def din(name, shape, dt=F32):
        return nc.dram_tensor(name, list(shape), dt, kind="ExternalInput").ap()
    xloc = din("xloc", [T, D])
    w_in = din("w_in", [D, 6144])
    w_sbo = din("w_sb_out", [512, D])
    w_hgo = din("w_hg_out", [512, D])
    w_out = din("w_out", [D, D])
    ngT_d = din("ngT", [128, 8])
    hgn_d = din("hgn_bc", [128, 512])
    fng_d = din("fng_bc", [128, D])
    lbl_d = din("lbl_bc", [128, 2, 512])
    lblT_d = din("lblT", [128, 2, 4])
    ident_d = din("ident", [128, 128], BF16)
    triC_d = din("triC", [128, 128])
    triSU_d = din("triSU", [128, 128])
    triK_d = din("triK", [128, 128], BF16)
    ones_d = din("ones", [128, 128], BF16)
    dmask_d = din("dmask", [128, 4, 512], BF16)
    out_d = nc.dram_tensor("out", [NOWN * 512, D], F32, kind="ExternalOutput").ap()

    with ExitStack() as es:
        S = Sync(nc, es)
        OP = S.op

        es_mem = ExitStack()

        def sb(name, shape, dt, scope=es_mem):
            return scope.enter_context(nc.sbuf_tensor(name, list(shape), dt))

        PS01 = es_mem.enter_context(nc.psum_tensor("ps01", [128, 2, 512], F32))
        PS23 = es_mem.enter_context(nc.psum_tensor("ps23", [128, 2, 512], F32))
        PS = [PS01[:, 0, :], PS01[:, 1, :], PS23[:, 0, :], PS23[:, 1, :]]
        PS += [es_mem.enter_context(nc.psum_tensor("ps%d" % i, [128, 512], F32))[:] for i in range(4, 7)]
        PTB = es_mem.enter_context(nc.psum_tensor("ptb", [128, 8, 128], BF16))
        PSZ = [PS01, PS23]

        def pk(i):
            return ("ps", i)

        def cload(name, shape, dt, src, scope=None):
            scope = es_mem if scope is None else scope
            tl = sb(name + "_s", shape, dt, scope)
            S.dma("c_" + name, [], [name], tl[:], src)
            return tl

        ident = cload("ident", [128, 128], BF16, ident_d)
        triC = cload("triC", [128, 128], F32, triC_d)
        triSU = cload("triSU", [128, 128], F32, triSU_d)
        triK = cload("triK", [128, 128], BF16, triK_d)
        ones = cload("ones", [128, 128], BF16, ones_d)
        dmask = cload("dmask", [128, 4, 512], BF16, dmask_d)
        ngT = cload("ngT", [128, 8], F32, ngT_d)

        yT = sb("yT", [128, 4, NOWN * 512], BF16)
        onall = sb("onall", [128, NOWN * 4, 512], BF16)

        NXS = 3
        xt = [sb("xt%d" % i, [128, D], F32) for i in range(NXS)]
        junkd = sb("junkd", [128, D], BF16)
        xs = [sb("xs%d" % i, [128, D], BF16) for i in range(2)]
        uT = [sb("uT%d" % i, [128, 8, 513], BF16) for i in range(2)]
        st = [sb("st%d" % i, [128, 4], F32) for i in range(NXS)]
        NWS = 4
        WCH = 512
        wst = [sb("wst%d" % i, [128, WCH], F32) for i in range(NWS)]
        wst_i = [0]
        x_i = [0]

        def load_w(dst, dkey, src, row0, col0, ncols, gcol=None):
            i = wst_i[0] % NWS
            on_act = (wst_i[0] % 2) == 1
            wst_i[0] += 1
            if dkey not in S.alias:
                S.alias[dkey] = [(dkey, 0), (dkey, 1)]
            dkey = (dkey, 1 if on_act else 0)
            S.dma("wst%d" % i, [], ["wst%d" % i], wst[i][:, 0:ncols], src[row0:row0 + 128, col0:col0 + ncols])
            if gcol is None:
                if on_act:
                    OP("act", "copy", ["wst%d" % i], [dkey], out=dst, in_=wst[i][:, 0:ncols])
                else:
                    OP("dve", "tensor_copy", ["wst%d" % i], [dkey], out=dst, in_=wst[i][:, 0:ncols])
            else:
                if on_act:
                    OP("act", "activation", ["wst%d" % i, "ngT"], [dkey], out=dst, in_=wst[i][:, 0:ncols],
                       func=AF.Copy, scale=gcol)
                else:
                    OP("dve", "tensor_scalar", ["wst%d" % i, "ngT"], [dkey], out=dst, in0=wst[i][:, 0:ncols],
                       scalar1=gcol, scalar2=None, op0=ALU.mult)

        def load_win(Wt, wkey, off, col0, ncols):
            for c in range(8):
                for q0 in range(0, ncols, WCH):
                    n = min(WCH, ncols - q0)
                    load_w(Wt[:, c, off + q0:off + q0 + n], wkey, w_in, c * 128, col0 + q0, n, gcol=ngT[:, c:c + 1])

        mhalf = sb("mhalf", [128, 1], F32)
        OP("pool", "memset", [], ["mhalf"], mhalf[:], -0.5)

        def rstd_chain(src, srckey, stt, sttkey, n=D, noact=False):
            OP("dve", "scalar_tensor_tensor", [srckey], [sttkey], out=junkd[:, 0:n], in0=src, scalar=1.0, in1=src,
               op0=ALU.mult, op1=ALU.mult, accum_out=stt[:, 0:1])
            OP("dve", "tensor_scalar", [sttkey], [sttkey], out=stt[:, 1:2], in0=stt[:, 0:1], scalar1=1.0 / n,
               scalar2=EPS, op0=ALU.mult, op1=ALU.add)
            if noact:
                OP("pool", "tensor_tensor", [sttkey, "mhalf"], [sttkey], out=stt[:, 3:4], in0=stt[:, 1:2], in1=mhalf[:],
                   op=ALU.pow)
                return
            OP("act", "activation", [sttkey], [sttkey], out=stt[:, 2:3], in_=stt[:, 1:2], func=AF.Ln)
            OP("act", "activation", [sttkey], [sttkey], out=stt[:, 3:4], in_=stt[:, 2:3], func=AF.Exp, scale=-0.5)

        def load_x(row0):
            i = x_i[0] % NXS
            x_i[0] += 1
            S.dma("x%d" % i, [], ["xt%d" % i], xt[i][:], xloc[row0:row0 + 128, :])
            return i

        def token_group_parts(g, gi, evac="act"):
            ub = gi % 2
            ut = uT[ub]
            ukey = "uT%d" % ub
            slot = {}

            def st_load(tb):
                slot[tb] = load_x(g * 512 + tb * 128)

            def st_stats(tb):
                i = slot[tb]
                rstd_chain(xt[i][:], "xt%d" % i, st[i], "st%d" % i,
                           noact=((evac() if callable(evac) else evac) != "act"))

            def st_rest(tb):
                i = slot[tb]
                j = tb % 2
                OP("dve", "tensor_scalar", ["xt%d" % i, "st%d" % i], ["xs%d" % j], out=xs[j][:], in0=xt[i][:],
                   scalar1=st[i][:, 3:4], scalar2=None, op0=ALU.mult)
                for c in range(8):
                    OP("pe", "transpose", ["xs%d" % j, "ident"], ["ptb"], out=PTB[:, c, :],
                       in_=xs[j][:, c * 128:(c + 1) * 128], identity=ident[:])
                if (evac() if callable(evac) else evac) == "act":
                    OP("act", "copy", ["ptb"], [ukey], out=ut[:, :, 1 + tb * 128:1 + (tb + 1) * 128], in_=PTB[:])
                else:
                    OP("dve", "tensor_copy", ["ptb"], [ukey], out=ut[:, :, 1 + tb * 128:1 + (tb + 1) * 128], in_=PTB[:])
            def part(tb):
                if tb == 0:
                    st_load(0)
                    st_load(1)
                    st_stats(0)
                if tb + 2 < 4:
                    st_load(tb + 2)
                if tb + 1 < 4:
                    st_stats(tb + 1)
                st_rest(tb)
                if tb == 3:
                    if gi == 0:
                        OP("pool", "memset", [], [ukey], ut[:, :, 0:1], 0.0)
                    else:
                        pv = uT[(gi - 1) % 2]
                        OP("pool", "tensor_copy", ["uT%d" % ((gi - 1) % 2)], [ukey], out=ut[:, :, 0:1],
                           in_=pv[:, :, 512:513])
            return ut, ukey, [(lambda tb=tb: part(tb)) for tb in range(4)]

        def token_group(g, gi):
            ut, ukey, parts = token_group_parts(g, gi)
            for p_ in parts:
                p_()
            return ut, ukey

        def interleave(nxt, quarters):
            for q in range(4):
                if nxt is not None:
                    nxt[2][q]()
                for f in quarters[q]:
                    f()

        def proj_fm(bank, Wt, wkey, c0, ut, ukey, shift=0, ncols=128):
            for c in range(8):
                OP("pe", "matmul", [wkey, ukey], [pk(bank)], PS[bank][0:ncols, :], lhsT=Wt[:, c, c0:c0 + ncols],
                   rhs=ut[:, c, 1 - shift:513 - shift], start=(c == 0), stop=(c == 7))

        def proj_tm(bank, src, skey, off, tb, Wt, wkey, c0, ncols):
            for c in range(8):
                OP("pe", "matmul", [wkey, skey], [pk(bank)], PS[bank][:, 0:ncols],
                   lhsT=src[:, c, off + tb * 128:off + (tb + 1) * 128], rhs=Wt[:, c, c0:c0 + ncols],
                   start=(c == 0), stop=(c == 7))

        def recip1p(tl, key):
            OP("act", "activation", [key], [key], out=tl, in_=tl, func=AF.Ln, bias=1.0)
            OP("act", "activation", [key], [key], out=tl, in_=tl, func=AF.Exp, scale=-1.0)

        SKIP = os.environ.get('KSKIP', '')
        def pass_H():
            with ExitStack() as hs:
                WH = sb("WH", [128, 8, 1536], BF16, hs)
                load_win(WH, "WH", 0, C_HGQ, 1536)
                hgn = cload("hgn", [128, 512], F32, hgn_d, hs)
                lbl = cload("lbl", [128, 2, 512], F32, lbl_d, hs)
                lblT = cload("lblT", [128, 2, 4], F32, lblT_d, hs)
                oml = sb("oml", [128, 512], F32, hs)
                omlT = sb("omlT", [128, 4], F32, hs)
                for (dst, dk, src, sk) in ((oml, "oml", lbl, "lbl"), (omlT, "omlT", lblT, "lblT")):
                    OP("dve", "tensor_tensor", [sk], [dk], out=dst[:], in0=src[:, 0, :], in1=src[:, 1, :], op=ALU.subtract)
                    OP("act", "activation", [dk], [dk], out=dst[:], in_=dst[:], func=AF.Exp)
                    recip1p(dst[:], dk)

                tA = [sb("h_tA%d" % i, [128, 512], F32, hs) for i in range(4)]
                tB = [sb("h_tB%d" % i, [128, 512], F32, hs) for i in range(4)]
                tC = [sb("h_tC%d" % i, [128, 512], F32, hs) for i in range(4)]
                tD = [sb("h_tD%d" % i, [128, 512], F32, hs) for i in range(4)]
                kk = [sb("h_kk%d" % i, [128, 512], F32, hs) for i in range(4)]
                gtok = sb("h_g", [128, 4, 512], F32, hs)
                kdl2 = [sb("h_kdl%d" % i, [128, 4, 512], BF16, hs) for i in range(2)]
                vc2 = [sb("h_vc%d" % i, [128, 4, 512], BF16, hs) for i in range(2)]
                kdT2 = [sb("h_kdT%d" % i, [128, 4, 512], BF16, hs) for i in range(2)]
                dec2 = [sb("h_dec%d" % i, [128, 4, 8], F32, hs) for i in range(2)]
                qz = sb("h_qz", [128, 4, 4, 2, 128], BF16, hs)
                Sst = sb("h_S", [128, 4, 128], F32, hs)
                Sbf = sb("h_Sbf", [128, 2, 4, 2, 128], BF16, hs)
                atm4 = [sb("h_atm%d" % i, [128, 4, 128], BF16, hs) for i in range(2)]
                osb4 = [sb("h_o%d" % i, [128, 512], F32, hs) for i in range(2)]
                ost4 = [sb("h_ost%d" % i, [128, 16], F32, hs) for i in range(2)]
                triC4 = sb("h_triC4", [128, 4, 128], F32, hs)
                for h in range(4):
                    OP("pool", "tensor_copy", ["triC"], ["triC4"], out=triC4[:, h, :], in_=triC[:])
                OP("pool", "memset", [], [("qz", h) for h in range(4)], qz[:], 0.0)
                OP("pool", "memset", [], [("S", h) for h in range(4)], Sst[:], 0.0)

                cur = token_group_parts(0, 0)
                if 'H' not in SKIP:
                    for p_ in cur[2]:
                        p_()
                for g in range(NG if 'H' not in SKIP else 0):
                    own = g in OWN
                    oj = OWN.index(g) if own else -1
                    ut, ukey = cur[0], cur[1]
                    nxt = token_group_parts(g + 1, g + 1) if g + 1 < NG else None
                    gp = g % 2
                    kdl, vc, kdT, dec = kdl2[gp], vc2[gp], kdT2[gp], dec2[gp]
                    K_vc = lambda tb: ("vc", gp, tb)
                    K_kdl = lambda tb: ("kdl", gp, tb)
                    K_kdT = lambda h: ("kdT", gp, h)
                    K_dec = lambda h: ("dec", gp, h)
                    def part_T1():
                        for tb in range(4):
                            proj_tm(0, ut, ukey, 1, tb, WH, "WH", 512, 512)
                            proj_tm(1, ut, ukey, 1, tb, WH, "WH", 1024, 512)
                            ka = "tA%d" % tb
                            OP("act", "activation", [pk(0)], [ka], out=tA[tb][:], in_=PS[0][:], func=AF.Exp)
                            OP("act", "copy", [pk(1)], [K_vc(tb)], out=vc[:, tb, :], in_=PS[1][:])
                            recip1p(tA[tb][:], ka)
                            OP("dve", "tensor_tensor", [ka, "oml"], ["kk%d" % tb], out=kk[tb][:], in0=tA[tb][:], in1=oml[:],
                               op=ALU.mult)
                    def part_T23():
                        for tb in range(4):
                            OP("act", "activation", ["kk%d" % tb], [("gtok", tb)], out=gtok[:, tb, :], in_=kk[tb][:], func=AF.Ln,
                               scale=-1.0, bias=1.0)
                        for tb in range(4):
                            OP("pe", "matmul", ["triSU", ("gtok", tb)], [pk(2)], PS[2][:], lhsT=triSU[:], rhs=gtok[:, tb, :],
                               start=True, stop=True)
                            OP("act", "activation", [pk(2)], ["tB%d" % tb], out=tB[tb][:], in_=PS[2][:], func=AF.Exp)
                            OP("dve", "tensor_tensor", ["kk%d" % tb, "tB%d" % tb], [K_kdl(tb)], out=kdl[:, tb, :], in0=kk[tb][:],
                               in1=tB[tb][:], op=ALU.mult)
                            for h in range(4):
                                OP("pe", "matmul", ["triC", ("gtok", tb)], [pk(3 + h)], PS[3 + h][:, tb * 128:(tb + 1) * 128],
                                   lhsT=gtok[:, tb, h * 128:(h + 1) * 128], rhs=triC[:], start=True, stop=True)
                    def part_F():
                        for h in range(4):
                            proj_fm(0, WH, "WH", 512 + h * 128, ut, ukey)
                            ka, kb_ = "tA%d" % h, "tB%d" % h
                            OP("act", "activation", [pk(0)], [ka], out=tA[h][:], in_=PS[0][:], func=AF.Exp)
                            recip1p(tA[h][:], ka)
                            OP("act", "activation", [pk(3 + h)], [kb_], out=tB[h][:], in_=PS[3 + h][:], func=AF.Exp, scale=-1.0)
                            OP("dve", "scalar_tensor_tensor", [ka, kb_, "omlT"], [K_kdT(h)], out=kdT[:, h, :], in0=tA[h][:],
                               scalar=omlT[:, h:h + 1], in1=tB[h][:], op0=ALU.mult, op1=ALU.mult)
                            OP("act", "activation", [pk(3 + h)], [K_dec(h)], out=dec[:, h, :],
                               in_=PS[3 + h][:].rearrange("p (c t) -> p c t", t=64)[:, :, 63], func=AF.Exp)
                        if own:
                            for h in range(4):
                                qb = h % 2
                                kc, kd = "tC%d" % h, "tD%d" % h
                                proj_fm(qb, WH, "WH", h * 128, ut, ukey)
                                OP("act", "activation", [pk(qb)], [kc], out=tC[h][:], in_=PS[qb][:], func=AF.Exp, scale=-1.0)
                                recip1p(tC[h][:], kc)
                                OP("dve", "tensor_tensor", [pk(qb), kc], [kc], out=tC[h][:], in0=PS[qb][:], in1=tC[h][:],
                                   op=ALU.mult)
                                OP("act", "activation", [pk(3 + h)], [kd], out=tD[h][:], in_=PS[3 + h][:], func=AF.Exp)
                                for ci in range(2):
                                    v4c = tC[h][:].rearrange("p (b c t) -> p b c t", b=4, c=2)[:, :, ci, :]
                                    v4d = tD[h][:].rearrange("p (b c t) -> p b c t", b=4, c=2)[:, :, ci, :]
                                    OP("dve", "scalar_tensor_tensor", [kc, kd], [("qz", h)],
                                       out=qz[:, h, :, ci, ci * 64:(ci + 1) * 64], in0=v4c, scalar=HG_SCALE, in1=v4d,
                                       op0=ALU.mult, op1=ALU.mult)
                    def part_R():
                        for cch in range(8):
                            tb, ci = cch // 2, cch % 2
                            for h in range(4):
                                hc = slice(h * 128, (h + 1) * 128)
                                rows = slice(ci * 64, (ci + 1) * 64)
                                if own:
                                    OP("pool", "tensor_copy", [("S", h)], [("Sbf", h, tb % 2)], out=Sbf[:, tb % 2, h, ci, :],
                                       in_=Sst[:, h, :])
                                OP("pe", "matmul", [K_kdl(tb), K_vc(tb)], [pk(3 + h)], PS[3 + h][:, 0:128], lhsT=kdl[rows, tb, hc],
                                   rhs=vc[rows, tb, hc], start=True, stop=True)
                                OP("dve", "scalar_tensor_tensor", [("S", h), K_dec(h), pk(3 + h)], [("S", h)], out=Sst[:, h, :],
                                   in0=Sst[:, h, :], scalar=dec[:, h, cch:cch + 1], in1=PS[3 + h][:, 0:128],
                                   op0=ALU.mult, op1=ALU.add)
                            if own and ci == 1:
                                blk = slice(tb * 128, (tb + 1) * 128)
                                pb = tb % 2
                                atm_, osb_, ost_ = atm4[pb], osb4[pb], ost4[pb]
                                ak, okk, sk = "atm%d" % pb, "osb%d" % pb, "ost%d" % pb
                                for h in range(4):
                                    reg = PS[2][:, h * 128:(h + 1) * 128]
                                    OP("pe", "matmul", [K_kdT(h), ("qz", h)], [pk(2)], reg, lhsT=kdT[:, h, blk],
                                       rhs=qz[:, h, tb, 0, :], start=True, stop=False)
                                    OP("pe", "matmul", [K_kdT(h), ("qz", h)], [pk(2)], reg, lhsT=kdT[:, h, blk],
                                       rhs=qz[:, h, tb, 1, :], start=False, stop=True)
                                OP("dve", "tensor_tensor", [pk(2), "triC4"], [ak], out=atm_[:],
                                   in0=PS[2][:].rearrange("p (h t) -> p h t", h=4), in1=triC4[:], op=ALU.mult)
                                for h in range(4):
                                    hc = slice(h * 128, (h + 1) * 128)
                                    reg = PS[1][:, hc]
                                    OP("pe", "matmul", [ak, K_vc(tb)], [pk(1)], reg, lhsT=atm_[:, h, :], rhs=vc[:, tb, hc],
                                       start=True, stop=False)
                                    OP("pe", "matmul", [("qz", h), ("Sbf", h, pb)], [pk(1)], reg, lhsT=qz[:, h, tb, 0, :],
                                       rhs=Sbf[:, pb, h, 0, :], start=False, stop=False)
                                    OP("pe", "matmul", [("qz", h), ("Sbf", h, pb)], [pk(1)], reg, lhsT=qz[:, h, tb, 1, :],
                                       rhs=Sbf[:, pb, h, 1, :], start=False, stop=True)
                                OP("act", "copy", [pk(1)], [okk], out=osb_[:], in_=PS[1][:])
                                for h in range(4):
                                    hc = slice(h * 128, (h + 1) * 128)
                                    OP("dve", "scalar_tensor_tensor", [okk], [sk], out=junkd[:, 0:128], in0=osb_[:, hc],
                                       scalar=1.0, in1=osb_[:, hc], op0=ALU.mult, op1=ALU.mult, accum_out=ost_[:, h:h + 1])
                                OP("dve", "tensor_scalar", [sk], [sk], out=ost_[:, 4:8], in0=ost_[:, 0:4], scalar1=1.0 / 128,
                                   scalar2=EPS, op0=ALU.mult, op1=ALU.add)
                                OP("act", "activation", [sk], [sk], out=ost_[:, 8:12], in_=ost_[:, 4:8], func=AF.Ln)
                                OP("act", "activation", [sk], [sk], out=ost_[:, 12:16], in_=ost_[:, 8:12], func=AF.Exp,
                                   scale=-0.5)
                                for h in range(4):
                                    hc = slice(h * 128, (h + 1) * 128)
                                    OP("dve", "scalar_tensor_tensor", [okk, sk, "hgn"], ["onall"],
                                       out=onall[:, oj * 4 + tb, hc], in0=osb_[:, hc], scalar=ost_[:, 12 + h:13 + h],
                                       in1=hgn[:, hc], op0=ALU.mult, op1=ALU.mult)
                    interleave(nxt, [[part_T1], [part_T23], [part_F], [part_R]])
                    cur = nxt
                S.barrier_all()


        def pass_A():
            with ExitStack() as as_:
                WA = sb("WA", [128, 8, 768], BF16, as_)
                KT = sb("KT", [128, 2, T], BF16, as_)
                DV = sb("DV", [128, NB, 256], BF16, as_)
                QT = sb("QT", [128, 2, NOWN, 512], BF16, as_)
                VsT = sb("VsT", [128, 2, NOWN, 512], BF16, as_)
                duT = sb("duT", [128, 8, 512], BF16, as_)
                Et = [sb("E%d" % p, [128, 2, 512], F32, as_) for p in range(2)]
                spt = [sb("sp%d" % p, [128, 2, 512], BF16, as_) for p in range(2)]
                Pt = [sb("P%d" % p, [128, 2, 512], BF16, as_) for p in range(2)]
                Rt = [sb("R%d" % p, [128, 2, 512], BF16, as_) for p in range(2)]

                for hp in range(2 if 'A' not in SKIP else 0):
                    h0 = hp * 4
                    load_win(WA, "WA", 0, C_SBQ + h0 * 64, 256)
                    load_win(WA, "WA", 256, C_SBK + h0 * 64, 256)
                    load_win(WA, "WA", 512, C_SBV + h0 * 64, 256)
                    bank_i = [0]
                    sweeping = [False]

                    def nb():
                        if sweeping[0]:
                            return 6
                        bank_i[0] = (bank_i[0] + 1) % 5
                        return (0, 1, 2, 3, 6)[bank_i[0]]

                    ev = lambda: ("dve" if sweeping[0] else "act")
                    tg = [token_group_parts(g, g, evac=ev) for g in range(NG)]
                    items = list(tg[0][2])
                    end_idx = {}
                    for g in range(NG):
                        own = g in OWN
                        oj = OWN.index(g) if own else -1
                        ut, ukey = tg[g][0], tg[g][1]

                        def w_du(ut=ut, ukey=ukey):
                            OP("dve", "tensor_tensor", [ukey], ["duT"], out=duT[:], in0=ut[:, :, 0:512], in1=ut[:, :, 1:513],
                               op=ALU.subtract)

                        def w_kt(p, g=g, ut=ut, ukey=ukey):
                            b = nb()
                            proj_fm(b, WA, "WA", 256 + p * 128, ut, ukey)
                            if sweeping[0]:
                                OP("dve", "tensor_copy", [pk(b)], [("KT", g)], out=KT[:, p, g * 512:(g + 1) * 512],
                                   in_=PS[b][:])
                            else:
                                OP("act", "copy", [pk(b)], [("KT", g)], out=KT[:, p, g * 512:(g + 1) * 512], in_=PS[b][:])

                        def w_dv(tb, g=g):
                            b = nb()
                            proj_tm(b, duT, "duT", 0, tb, WA, "WA", 512, 256)
                            OP("dve", "tensor_copy", [pk(b)], [("DV", g)], out=DV[:, g * 4 + tb, :], in_=PS[b][:, 0:256])

                        def w_own(p, oj=oj, ut=ut, ukey=ukey):
                            b = nb()
                            proj_fm(b, WA, "WA", p * 128, ut, ukey)
                            OP("dve", "tensor_copy", [pk(b)], [("QT", oj)], out=QT[:, p, oj, :], in_=PS[b][:])
                            b = nb()
                            proj_fm(b, WA, "WA", 512 + p * 128, ut, ukey, shift=1)
                            OP("dve", "tensor_copy", [pk(b)], [("VsT", oj)], out=VsT[:, p, oj, :], in_=PS[b][:])
                        quarters = [[w_du, (lambda f=w_kt: f(0)), (lambda f=w_kt: f(1))],
                                    [(lambda f=w_dv: f(0)), (lambda f=w_dv: f(1))],
                                    [(lambda f=w_dv: f(2)), (lambda f=w_dv: f(3))],
                                    ([(lambda f=w_own: f(0)), (lambda f=w_own: f(1))] if own else [])]
                        for q in range(4):
                            if g + 1 < NG:
                                items.append(tg[g + 1][2][q])
                            items.extend(quarters[q])
                        end_idx[g] = len(items)

                    def zmm(oj, p, kb):
                        for i in range(2):
                            rows = slice(i * 64, (i + 1) * 64)
                            OP("pe", "matmul", [("KT", kb // 4), ("QT", oj)], [pk(2 * p + i)], PSZ[p][:, i, :],
                               lhsT=KT[rows, p, kb * 128:(kb + 1) * 128], rhs=QT[rows, p, oj, :], start=True, stop=True)

                    def sweep_step(oj, g, stp, nsteps):
                        kb = 4 * g + 3 - stp
                        diag = stp < 4
                        cp = 3 - stp
                        for p in range(2):
                            for i in range(2):
                                OP("act", "activation", [pk(2 * p + i)], ["E%d%d" % (p, i)], out=Et[p][:, i, :],
                                   in_=PSZ[p][:, i, :], func=AF.Exp, scale=SB_SCALE)
                        for p in range(2):
                            for i in range(2):
                                spk = "sp%d%d" % (p, i)
                                OP("act", "activation", ["E%d%d" % (p, i)], [spk], out=spt[p][:, i, :], in_=Et[p][:, i, :],
                                   func=AF.Ln, bias=1.0)
                                if diag:
                                    OP("dve", "tensor_tensor", [spk, "dmask"], [spk], out=spt[p][:, i, :],
                                       in0=spt[p][:, i, :], in1=dmask[:, cp, :], op=ALU.mult)
                        for p in range(2):
                            for i in range(2):
                                OP("pe", "matmul", ["triK", "sp%d%d" % (p, i)], [pk(2 * p + i)], PSZ[p][:, i, :], lhsT=triK[:],
                                   rhs=spt[p][:, i, :], start=True, stop=(stp == 0))
                                if stp > 0:
                                    OP("pe", "matmul", ["ones", "R%d" % p], [pk(2 * p + i)], PSZ[p][:, i, :], lhsT=ones[:],
                                       rhs=Rt[p][:, i, :], start=False, stop=True)
                        for p in range(2):
                            for i in range(2):
                                pkk = "P%d%d" % (p, i)
                                OP("act", "activation", [pk(2 * p + i)], [pkk], out=Pt[p][:, i, :], in_=PSZ[p][:, i, :],
                                   func=AF.Exp, scale=-1.0)
                                if diag:
                                    OP("dve", "tensor_tensor", [pkk, "dmask"], [pkk], out=Pt[p][:, i, :],
                                       in0=Pt[p][:, i, :], in1=dmask[:, cp, :], op=ALU.mult)
                        for p in range(2):
                            if stp + 1 < nsteps:
                                zmm(oj, p, kb - 1)
                        for p in range(2):
                            for i in range(2):
                                hcol = slice((p * 2 + i) * 64, (p * 2 + i + 1) * 64)
                                OP("pe", "matmul", [("DV", kb // 4), "P%d%d" % (p, i)], [pk(4 + p)],
                                   PS[4 + p][i * 64:(i + 1) * 64, :], lhsT=DV[:, kb, hcol], rhs=Pt[p][:, i, :],
                                   start=(stp == 0), stop=(stp == nsteps - 1))
                        if stp + 1 < nsteps:
                            for p in range(2):
                                if stp == 0:
                                    OP("dve", "tensor_copy", ["sp%d0" % p, "sp%d1" % p], ["R%d" % p], out=Rt[p][:],
                                       in_=spt[p][:])
                                else:
                                    OP("dve", "tensor_tensor", ["sp%d0" % p, "sp%d1" % p, "R%d" % p], ["R%d" % p],
                                       out=Rt[p][:], in0=Rt[p][:], in1=spt[p][:], op=ALU.add)

                    it = 0
                    OWNS = OWN if 'S' not in SKIP else []
                    for oj, g in enumerate(OWNS):
                        while it < end_idx[g]:
                            items[it]()
                            it += 1
                        sweeping[0] = True
                        limit = end_idx[OWNS[oj + 1]] if oj + 1 < len(OWNS) else len(items)
                        n_items = limit - it
                        nsteps = 4 * g + 4
                        done = 0
                        for p in range(2):
                            zmm(oj, p, 4 * g + 3)
                        for stp in range(nsteps):
                            sweep_step(oj, g, stp, nsteps)
                            want = ((stp + 1) * n_items) // nsteps
                            while done < want:
                                items[it]()
                                it += 1
                                done += 1
                        for p in range(2):
                            pg = hp * 2 + p
                            OP("dve", "tensor_tensor", [pk(4 + p), ("VsT", oj)], [("yT", pg, oj)],
                               out=yT[:, pg, oj * 512:(oj + 1) * 512], in0=PS[4 + p][:], in1=VsT[:, p, oj, :], op=ALU.add)
                    while it < len(items):
                        items[it]()
                        it += 1
                S.barrier_all()


        for _ph in os.environ.get('KORDER', 'AH'):
            (pass_A if _ph == 'A' else pass_H)()

        with ExitStack() as cs:
            WC = sb("WC", [128, 8, 3072], BF16, cs)
            load_win(WC, "WC", 0, C_SBG, 512)
            load_win(WC, "WC", 512, C_HGG, 512)
            load_win(WC, "WC", 1024, C_GSB, 1024)
            load_win(WC, "WC", 2048, C_GHG, 1024)
            Wso = sb("Wso", [128, 4, D], BF16, cs)
            Who = sb("Who", [128, 4, D], BF16, cs)
            Wo = sb("Wo", [128, 8, D], BF16, cs)
            for q in range(4):
                for q0 in range(0, 1024, WCH):
                    load_w(Wso[:, q, q0:q0 + WCH], "Wso", w_sbo, q * 128, q0, WCH)
                    load_w(Who[:, q, q0:q0 + WCH], "Who", w_hgo, q * 128, q0, WCH)
            for c in range(8):
                for q0 in range(0, 1024, WCH):
                    load_w(Wo[:, c, q0:q0 + WCH], "Wo", w_out, c * 128, q0, WCH)
            fng = cload("fng", [128, D], F32, fng_d, cs)
            ysbT = sb("ysbT", [128, 4, 512], BF16, cs)
            yhgT = sb("yhgT", [128, 4, 512], BF16, cs)
            yhg2 = [sb("yhg%d" % i, [128, 512], BF16, cs) for i in range(2)]
            mT = sb("mT", [128, 8, 512], BF16, cs)
            c1b = [sb("c1_%d" % i, [128, 512], F32, cs) for i in range(2)]
            c2b = [sb("c2_%d" % i, [128, 512], F32, cs) for i in range(2)]
            hnb = [sb("hn%d" % i, [128, D], F32, cs) for i in range(2)]
            stc = sb("stc", [128, 4], F32, cs)

            def sigmoid_from(bank, tmp, tkey):
                OP("act", "activation", [pk(bank)], [tkey], out=tmp[:], in_=PS[bank][:], func=AF.Exp, scale=-1.0)
                recip1p(tmp[:], tkey)

            OWNC = OWN if 'C' not in SKIP else []
            cur = token_group_parts(OWNC[0], 0) if OWNC else None
            if cur is not None:
                for p_ in cur[2]:
                    p_()
            for oj, g in enumerate(OWNC):
                ut, ukey = cur[0], cur[1]
                nxt = token_group_parts(OWNC[oj + 1], oj + 1) if oj + 1 < len(OWNC) else None
                def part_sb():
                    for pg in range(4):
                        b = pg % 2
                        c1, k1 = c1b[pg % 2], "c1_%d" % (pg % 2)
                        proj_fm(b, WC, "WC", pg * 128, ut, ukey)
                        sigmoid_from(b, c1, k1)
                        OP("dve", "tensor_tensor", [pk(b), k1], [k1], out=c1[:], in0=PS[b][:], in1=c1[:], op=ALU.mult)
                        OP("dve", "tensor_tensor", [k1, ("yT", pg, oj)], ["ysbT"], out=ysbT[:, pg, :], in0=c1[:],
                           in1=yT[:, pg, oj * 512:(oj + 1) * 512], op=ALU.mult)
                def part_hg():
                    for tb in range(4):
                        b = 2 + tb % 2
                        c2, k2 = c2b[tb % 2], "c2_%d" % (tb % 2)
                        yhg, ky = yhg2[tb % 2], "yhg%d" % (tb % 2)
                        proj_tm(b, ut, ukey, 1, tb, WC, "WC", 512, 512)
                        sigmoid_from(b, c2, k2)
                        OP("dve", "tensor_tensor", [pk(b), k2], [k2], out=c2[:], in0=PS[b][:], in1=c2[:], op=ALU.mult)
                        OP("dve", "tensor_tensor", [k2, "onall"], [ky], out=yhg[:], in0=c2[:],
                           in1=onall[:, oj * 4 + tb, :], op=ALU.mult)
                        for q in range(4):
                            OP("pe", "transpose", [ky, "ident"], ["ptb"], out=PTB[:, q, :], in_=yhg[:, q * 128:(q + 1) * 128],
                               identity=ident[:])
                        OP("act", "copy", ["ptb"], ["yhgT"], out=yhgT[:, :, tb * 128:(tb + 1) * 128], in_=PTB[:, 0:4, :])
                def w_cc(cc):
                    cs_ = slice(cc * 128, (cc + 1) * 128)
                    bs = [(4 * cc + k) % 7 for k in range(4)]
                    c1, k1 = c1b[cc % 2], "c1_%d" % (cc % 2)
                    c2, k2 = c2b[cc % 2], "c2_%d" % (cc % 2)
                    proj_fm(bs[0], WC, "WC", 1024 + cc * 128, ut, ukey)
                    proj_fm(bs[1], WC, "WC", 2048 + cc * 128, ut, ukey)
                    for q in range(4):
                        OP("pe", "matmul", ["Wso", "ysbT"], [pk(bs[2])], PS[bs[2]][:], lhsT=Wso[:, q, cs_], rhs=ysbT[:, q, :],
                           start=(q == 0), stop=(q == 3))
                    for q in range(4):
                        OP("pe", "matmul", ["Who", "yhgT"], [pk(bs[3])], PS[bs[3]][:], lhsT=Who[:, q, cs_], rhs=yhgT[:, q, :],
                           start=(q == 0), stop=(q == 3))
                    sigmoid_from(bs[0], c1, k1)
                    sigmoid_from(bs[1], c2, k2)
                    OP("dve", "tensor_tensor", [pk(bs[2]), k1], [k1], out=c1[:], in0=PS[bs[2]][:], in1=c1[:], op=ALU.mult)
                    OP("dve", "tensor_tensor", [pk(bs[3]), k2], [k2], out=c2[:], in0=PS[bs[3]][:], in1=c2[:], op=ALU.mult)
                    OP("dve", "tensor_tensor", [k1, k2], [("mT", cc)], out=mT[:, cc, :], in0=c1[:], in1=c2[:], op=ALU.add)
                def part_delta():
                    for tb in range(4):
                        o_i = tb % 2
                        hn = hnb[o_i]
                        hk = "hn%d" % o_i
                        xi = load_x(g * 512 + tb * 128)
                        for half in range(2):
                            bk = half
                            for c in range(8):
                                OP("pe", "matmul", [("mT", c), "Wo"], [pk(bk)], PS[bk][:], lhsT=mT[:, c, tb * 128:(tb + 1) * 128],
                                   rhs=Wo[:, c, half * 512:(half + 1) * 512], start=(c == 0), stop=(c == 7))
                            OP("dve", "tensor_tensor", [pk(bk), "xt%d" % xi], [hk], out=hn[:, half * 512:(half + 1) * 512],
                               in0=PS[bk][:], in1=xt[xi][:, half * 512:(half + 1) * 512], op=ALU.add)
                        rstd_chain(hn[:], hk, stc, "stc")
                        OP("dve", "scalar_tensor_tensor", [hk, "stc", "fng"], [hk], out=hn[:], in0=hn[:],
                           scalar=stc[:, 3:4], in1=fng[:], op0=ALU.mult, op1=ALU.mult)
                        row0 = oj * 512 + tb * 128
                        S.dma("out%d" % o_i, [hk], [], out_d[row0:row0 + 128, :], hn[:])
                interleave(nxt, [[part_sb], [part_hg], [lambda: [w_cc(c_) for c_ in range(4)]],
                                 [lambda: [w_cc(c_) for c_ in range(4, 8)], part_delta]])
                cur = nxt
            S.barrier_all()
        S.emit()
        es_mem.pop_all()
    return nc


NG_FULL = 17
OWN_FULL = [4, 8, 12, 16]


def make_in_maps(x, meta, norm_g, w_in, w_sb_out, w_hg_out, w_out, hg_norm_g, hg_lb_logits, final_norm_g,
                 NG=NG_FULL, OWN=OWN_FULL, ncores_per_batch=4):
    B = x.shape[0]
    consts = host_consts()
    f32 = np.float32
    shared = {
        "w_in": np.ascontiguousarray(w_in[0], f32),
        "w_sb_out": np.ascontiguousarray(w_sb_out[0], f32),
        "w_hg_out": np.ascontiguousarray(w_hg_out[0], f32),
        "w_out": np.ascontiguousarray(w_out[0], f32),
        "ngT": np.ascontiguousarray(norm_g[0].reshape(8, 128).T, f32),
        "hgn_bc": np.ascontiguousarray(np.broadcast_to(hg_norm_g[0][None, :], (128, 512)), f32),
        "fng_bc": np.ascontiguousarray(np.broadcast_to(final_norm_g[None, :], (128, D)), f32),
        "lbl_bc": np.ascontiguousarray(np.broadcast_to(hg_lb_logits[None, :, :], (128, 2, 512)), f32),
        "lblT": np.ascontiguousarray(hg_lb_logits.reshape(2, 4, 128).transpose(2, 0, 1), f32),
    }
    shared.update(consts)
    in_maps = []
    nmeta = meta.shape[0]
    for b in range(B):
        G = np.concatenate([np.zeros((512 - nmeta, D), f32), meta.astype(f32), x[b].astype(f32)], axis=0)
        for r in range(ncores_per_batch):
            nz = (ncores_per_batch - 1 - r) * 512
            nreal = NG * 512 - nz
            xl = np.concatenate([np.zeros((nz, D), f32), G[:nreal]], axis=0)
            m = dict(shared)
            m["xloc"] = np.ascontiguousarray(xl)
            in_maps.append(m)
    return in_maps


_NC_CACHE = {}


def kernel(x, meta, norm_g, w_in, w_sb_out, w_hg_out, w_out, hg_norm_g, hg_lb_logits, final_norm_g):
    x = np.asarray(x)
    B, SEQ, _ = x.shape
    in_maps = make_in_maps(x, np.asarray(meta), np.asarray(norm_g), np.asarray(w_in), np.asarray(w_sb_out),
                           np.asarray(w_hg_out), np.asarray(w_out), np.asarray(hg_norm_g),
                           np.asarray(hg_lb_logits), np.asarray(final_norm_g))
    key = (NG_FULL, tuple(OWN_FULL))
    if key not in _NC_CACHE:
        _NC_CACHE[key] = build(NG_FULL, OWN_FULL)
    nc = _NC_CACHE[key]
    res = run_bass_kernel_spmd(nc, in_maps, core_ids=list(range(len(in_maps))))
    out = np.empty((B, SEQ, D), np.float32)
    for b in range(B):
        for r in range(4):
            o = np.asarray(res.results[b * 4 + r]["out"])
            for j in range(4):
                row = (4 * j + r) * 512
                out[b, row:row + 512] = o[j * 512:(j + 1) * 512]
    return out
```
